# Optimizing a Trainium2 kernel written in Bass

```python
import math
import jax, jax.numpy as jnp
from jax import lax
import numpy as np

D_MODEL = 1024
BATCH = 8
SEQ = 4096
DEPTH = 1

ATTN_HEADS = 8
KV_HEADS = 2
HEAD_DIM = 64
GQA_GROUP = ATTN_HEADS // KV_HEADS
ATTN_WIDTH = ATTN_HEADS * HEAD_DIM
KV_WIDTH = KV_HEADS * HEAD_DIM
WINDOW = 128
ATTN_BLOCK = 128
ROPE_DIM = HEAD_DIM // 4
ROPE_THETA = 500000.0
RNN_WIDTH = D_MODEL // 2
RNN_HEADS = 8
RNN_HEAD_DIM = RNN_WIDTH // RNN_HEADS
CONV_WIDTH = 4
LRU_C = 8.0
MIX_WIDTH = ATTN_WIDTH + RNN_WIDTH
IN_WIDTH = ATTN_WIDTH + 2 * KV_WIDTH + 2 * RNN_WIDTH
PEER_HEADS = 8
PEER_N_KEYS = 128
PEER_N_EXPERTS = PEER_N_KEYS * PEER_N_KEYS
PEER_HALF_DIM = 128
PEER_TOPK = 16
PEER_CHUNK = 128
ALPHA = (2.0 * DEPTH) ** 0.25
BETA = (8.0 * DEPTH) ** -0.25
EPS = 1e-5

kernel_name = "hymba_hawk_swa_sink_peer_deepnorm"


def layer_norm(x, g, b):
    xf = x.astype(jnp.float32)
    mu = jnp.mean(xf, axis=-1, keepdims=True)
    var = jnp.mean(jnp.square(xf - mu), axis=-1, keepdims=True)
    return ((xf - mu) * lax.rsqrt(var + EPS)).astype(x.dtype) * g + b


def rms_norm(x, g):
    xf = x.astype(jnp.float32)
    return (xf * lax.rsqrt(jnp.mean(jnp.square(xf), axis=-1, keepdims=True) + EPS)).astype(x.dtype) * g


def partial_rotary(t, cos, sin):
    half = ROPE_DIM // 2
    tf = t.astype(jnp.float32)
    t1, t2, rest = tf[..., :half], tf[..., half:ROPE_DIM], tf[..., ROPE_DIM:]
    out = jnp.concatenate([t1 * cos - t2 * sin, t2 * cos + t1 * sin, rest], axis=-1)
    return out.astype(t.dtype)


def sliding_window_attention(q, k, v, sinks):
    B, S = q.shape[0], q.shape[1]
    nb = S // ATTN_BLOCK
    qb = q.reshape(B, nb, ATTN_BLOCK, KV_HEADS, GQA_GROUP, HEAD_DIM)
    kb = k.reshape(B, nb, ATTN_BLOCK, KV_HEADS, HEAD_DIM)
    vb = v.reshape(B, nb, ATTN_BLOCK, KV_HEADS, HEAD_DIM)
    k_prev = jnp.concatenate([jnp.zeros_like(kb[:, :1]), kb[:, :-1]], axis=1)
    v_prev = jnp.concatenate([jnp.zeros_like(vb[:, :1]), vb[:, :-1]], axis=1)
    kw = jnp.concatenate([k_prev, kb], axis=2)
    vw = jnp.concatenate([v_prev, vb], axis=2)
    s = jnp.einsum('bnqhgd,bnkhd->bnhgqk', qb, kw).astype(jnp.float32) * (HEAD_DIM ** -0.5)
    qi = jnp.arange(ATTN_BLOCK)[:, None]
    ki = jnp.arange(2 * ATTN_BLOCK)[None, :]
    diff = ATTN_BLOCK + qi - ki
    blk = jnp.arange(nb)[:, None, None]
    kpos = (blk - 1) * ATTN_BLOCK + ki[None]
    valid = (diff >= 0)[None] & (diff < WINDOW)[None] & (kpos >= 0)
    s = jnp.where(valid[None, :, None, None], s, -jnp.inf)
    sink = sinks.astype(jnp.float32).reshape(KV_HEADS, GQA_GROUP)[None, None, :, :, None, None]
    m = jnp.maximum(jnp.max(s, axis=-1, keepdims=True), sink)
    p = jnp.exp(s - m)
    denom = jnp.sum(p, axis=-1, keepdims=True) + jnp.exp(sink - m)
    probs = (p / denom).astype(v.dtype)
    o = jnp.einsum('bnhgqk,bnkhd->bnqhgd', probs, vw)
    return o.reshape(B, S, ATTN_WIDTH)


def rg_lru_branch(xr, gate_in, conv_w, conv_b, gate_a_w, gate_a_b, gate_x_w, gate_x_b, lru_lambda):
    B, S, C = xr.shape
    xc = lax.conv_general_dilated(
        xr, conv_w.reshape(CONV_WIDTH, 1, C), window_strides=(1,),
        padding=[(CONV_WIDTH - 1, 0)], dimension_numbers=('NWC', 'WIO', 'NWC'),
        feature_group_count=C) + conv_b
    xh = xc.reshape(B, S, RNN_HEADS, RNN_HEAD_DIM)
    r = jax.nn.sigmoid(jnp.einsum('bshi,hij->bshj', xh, gate_a_w).reshape(B, S, C) + gate_a_b)
    i = jax.nn.sigmoid(jnp.einsum('bshi,hij->bshj', xh, gate_x_w).reshape(B, S, C) + gate_x_b)
    log_a = -LRU_C * r.astype(jnp.float32) * jax.nn.softplus(-lru_lambda.astype(jnp.float32))
    a = jnp.exp(log_a)
    b = jnp.sqrt(-jnp.expm1(2.0 * log_a)) * (i * xc).astype(jnp.float32)

    def combine(left, right):
        a_l, b_l = left
        a_r, b_r = right
        return a_l * a_r, a_r * b_l + b_r

    _, h = lax.associative_scan(combine, (a, b), axis=1)
    return jax.nn.gelu(gate_in) * h.astype(xr.dtype)


def peer_ffn(h, peer_w_q, peer_keys_1, peer_keys_2, peer_u, peer_v):
    B, S, D = h.shape
    T = B * S
    ht = h.reshape(T, D)
    q = (ht @ peer_w_q).reshape(T, PEER_HEADS, 2, PEER_HALF_DIM)
    s1 = jnp.einsum('thd,kd->thk', q[:, :, 0], peer_keys_1)
    s2 = jnp.einsum('thd,kd->thk', q[:, :, 1], peer_keys_2)
    v1, i1 = lax.top_k(s1, PEER_TOPK)
    v2, i2 = lax.top_k(s2, PEER_TOPK)
    cand = (v1[..., :, None] + v2[..., None, :]).reshape(T, PEER_HEADS, PEER_TOPK * PEER_TOPK)
    sc, ci = lax.top_k(cand, PEER_TOPK)
    e1 = jnp.take_along_axis(i1, ci // PEER_TOPK, axis=-1)
    e2 = jnp.take_along_axis(i2, ci % PEER_TOPK, axis=-1)
    experts = (e1 * PEER_N_KEYS + e2).reshape(T, PEER_HEADS * PEER_TOPK)
    gates = jax.nn.softmax(sc.astype(jnp.float32), axis=-1).astype(h.dtype).reshape(T, PEER_HEADS * PEER_TOPK)
    n_chunks = T // PEER_CHUNK

    def chunk_fn(args):
        hc, ec, gc = args
        u = peer_u[ec]
        act = jax.nn.gelu(jnp.einsum('cd,ced->ce', hc, u))
        return jnp.einsum('ce,ced->cd', gc * act, peer_v[ec])

    out = lax.map(chunk_fn, (ht.reshape(n_chunks, PEER_CHUNK, D),
                             experts.reshape(n_chunks, PEER_CHUNK, -1),
                             gates.reshape(n_chunks, PEER_CHUNK, -1)))
    return out.reshape(B, S, D)


def setup_inputs(seed: int = 0) -> dict:
    key = jax.random.key(seed)
    ks = jax.random.split(key, 24)
    f32 = jnp.float32
    nrm = lambda k, shape, s: jax.random.normal(k, shape, f32) * s
    x = jax.random.normal(ks[0], (BATCH, SEQ, D_MODEL), f32)
    w_in = nrm(ks[1], (D_MODEL, IN_WIDTH), D_MODEL ** -0.5)
    v_lo = ATTN_WIDTH + KV_WIDTH
    w_in = w_in.at[:, v_lo:v_lo + KV_WIDTH].multiply(BETA)
    b_in = nrm(ks[2], (IN_WIDTH,), 0.01)
    attn_sinks = nrm(ks[3], (ATTN_HEADS,), 0.5)
    conv_w = nrm(ks[4], (CONV_WIDTH, RNN_WIDTH), CONV_WIDTH ** -0.5)
    conv_b = nrm(ks[5], (RNN_WIDTH,), 0.01)
    gate_a_w = nrm(ks[6], (RNN_HEADS, RNN_HEAD_DIM, RNN_HEAD_DIM), RNN_HEAD_DIM ** -0.5)
    gate_a_b = nrm(ks[7], (RNN_WIDTH,), 0.01)
    gate_x_w = nrm(ks[8], (RNN_HEADS, RNN_HEAD_DIM, RNN_HEAD_DIM), RNN_HEAD_DIM ** -0.5)
    gate_x_b = nrm(ks[9], (RNN_WIDTH,), 0.01)
    a_c = jax.random.uniform(ks[10], (RNN_WIDTH,), f32, 0.9, 0.999)
    a0 = a_c ** (1.0 / LRU_C)
    lru_lambda = jnp.log(a0) - jnp.log1p(-a0)
    norm_attn_g = 1.0 + nrm(ks[11], (ATTN_WIDTH,), 0.02)
    norm_rnn_g = 1.0 + nrm(ks[12], (RNN_WIDTH,), 0.02)
    w_out = nrm(ks[13], (MIX_WIDTH, D_MODEL), BETA * MIX_WIDTH ** -0.5)
    b_out = nrm(ks[14], (D_MODEL,), 0.01)
    ln1_g = 1.0 + nrm(ks[15], (D_MODEL,), 0.02)
    ln1_b = nrm(ks[16], (D_MODEL,), 0.01)
    peer_w_q = nrm(ks[17], (D_MODEL, PEER_HEADS * 2 * PEER_HALF_DIM), D_MODEL ** -0.5)
    peer_keys_1 = nrm(ks[18], (PEER_N_KEYS, PEER_HALF_DIM), PEER_HALF_DIM ** -0.5)
    peer_keys_2 = nrm(ks[19], (PEER_N_KEYS, PEER_HALF_DIM), PEER_HALF_DIM ** -0.5)
    peer_u = nrm(ks[20], (PEER_N_EXPERTS, D_MODEL), D_MODEL ** -0.5)
    peer_v = nrm(ks[21], (PEER_N_EXPERTS, D_MODEL), BETA * PEER_HEADS ** -0.5)
    ln2_g = 1.0 + nrm(ks[22], (D_MODEL,), 0.02)
    ln2_b = nrm(ks[23], (D_MODEL,), 0.01)
    return {"x": x, "w_in": w_in, "b_in": b_in, "attn_sinks": attn_sinks,
            "conv_w": conv_w, "conv_b": conv_b, "gate_a_w": gate_a_w, "gate_a_b": gate_a_b,
            "gate_x_w": gate_x_w, "gate_x_b": gate_x_b, "lru_lambda": lru_lambda,
            "norm_attn_g": norm_attn_g, "norm_rnn_g": norm_rnn_g, "w_out": w_out, "b_out": b_out,
            "ln1_g": ln1_g, "ln1_b": ln1_b, "peer_w_q": peer_w_q, "peer_keys_1": peer_keys_1,
            "peer_keys_2": peer_keys_2, "peer_u": peer_u, "peer_v": peer_v,
            "ln2_g": ln2_g, "ln2_b": ln2_b}


def reference(x, w_in, b_in, attn_sinks, conv_w, conv_b, gate_a_w, gate_a_b, gate_x_w, gate_x_b,
              lru_lambda, norm_attn_g, norm_rnn_g, w_out, b_out, ln1_g, ln1_b, peer_w_q,
              peer_keys_1, peer_keys_2, peer_u, peer_v, ln2_g, ln2_b):
    B, S, _ = x.shape
    pos = jnp.arange(S, dtype=jnp.float32)
    inv_freq = ROPE_THETA ** (-jnp.arange(0, ROPE_DIM, 2, dtype=jnp.float32) / ROPE_DIM)
    ang = pos[:, None] * inv_freq[None, :]
    cos = jnp.cos(ang)[None, :, None, :]
    sin = jnp.sin(ang)[None, :, None, :]

    h = x
    for _ in range(DEPTH):
        z = h @ w_in + b_in
        c0 = ATTN_WIDTH
        c1 = c0 + KV_WIDTH
        c2 = c1 + KV_WIDTH
        c3 = c2 + RNN_WIDTH
        q = z[..., :c0].reshape(B, S, ATTN_HEADS, HEAD_DIM)
        k = z[..., c0:c1].reshape(B, S, KV_HEADS, HEAD_DIM)
        v = z[..., c1:c2].reshape(B, S, KV_HEADS, HEAD_DIM)
        q = partial_rotary(q, cos, sin)
        k = partial_rotary(k, cos, sin)
        attn_out = sliding_window_attention(q, k, v, attn_sinks)
        rnn_out = rg_lru_branch(z[..., c2:c3], z[..., c3:], conv_w, conv_b, gate_a_w, gate_a_b,
                                gate_x_w, gate_x_b, lru_lambda)
        mixed = jnp.concatenate([rms_norm(attn_out, norm_attn_g), rms_norm(rnn_out, norm_rnn_g)], axis=-1)
        h = layer_norm(ALPHA * h + (mixed @ w_out + b_out), ln1_g, ln1_b)
        h = layer_norm(ALPHA * h + peer_ffn(h, peer_w_q, peer_keys_1, peer_keys_2, peer_u, peer_v), ln2_g, ln2_b)
    return h
```

```python
import os
import numpy as np
from contextlib import ExitStack
import concourse.bass as bass
import concourse.mybir as mybir
from concourse.bass_utils import run_bass_kernel_spmd

F32 = mybir.dt.float32
BF16 = mybir.dt.bfloat16
U32 = mybir.dt.uint32
AF = mybir.ActivationFunctionType
ALU = mybir.AluOpType
AX = mybir.AxisListType

S_LEN = 4096
D = 1024
NCORES = 8
ALPHA = 2.0 ** 0.25
EPS = 1e-5
NE1 = int(os.environ.get("K_NE1", "128"))
DEBUG = os.environ.get("K_DEBUG", "")

ENGS = ['pe', 'dve', 'act', 'pool', 'sp']
NDMA = 96
NDMA_HW = 64


class Sched:
    def __init__(self, nc, es):
        self.nc = nc
        self.sems = {e: es.enter_context(nc.semaphore("sem_" + e)) for e in ENGS}
        self.cnt = {e: 0 for e in ENGS}
        self.prog = {e: [] for e in ENGS}
        self.seen = {e: {f: 0 for f in ENGS} for e in ENGS}
        self.dseen = {e: {} for e in ENGS}
        self.lastw = {}
        self.rd = {}
        self.dma_sems = [es.enter_context(nc.semaphore("dsem%d" % i)) for i in range(NDMA)]
        self.dma_use = [0] * NDMA
        self.dma_next = 0
        self.dma_next_sw = 0
        self.out_tokens = []
        self.hook = None
        self._in_hook = False

    def _run_hook(self):
        if self.hook is not None and not self._in_hook:
            self._in_hook = True
            try:
                self.hook()
            finally:
                self._in_hook = False

    def _wait(self, eng, tok, kind, is_dma_issue=False):
        if tok[0] == 'e':
            _, e2, seq = tok
            if e2 == eng and not is_dma_issue:
                if eng == 'pe':
                    return
            if self.seen[eng][e2] >= seq:
                return
            self.seen[eng][e2] = seq
            sem = self.sems[e2]
            self.prog[eng].append(lambda E, sem=sem, seq=seq: E.wait_ge(sem, seq))
        else:
            _, idx, val = tok
            if self.dseen[eng].get(idx, 0) >= val:
                return
            self.dseen[eng][idx] = val
            sem = self.dma_sems[idx]
            self.prog[eng].append(lambda E, sem=sem, val=val: E.wait_ge(sem, val))

    def _deps(self, eng, reads, writes, is_dma_issue=False):
        for r in reads:
            t = self.lastw.get(r)
            if t is not None:
                self._wait(eng, t, 'raw', is_dma_issue)
        for w in writes:
            t = self.lastw.get(w)
            if t is not None:
                self._wait(eng, t, 'waw', is_dma_issue)
            for t in self.rd.get(w, ()):
                self._wait(eng, t, 'war', is_dma_issue)

    def _record(self, tok, reads, writes):
        for r in reads:
            lst = self.rd.setdefault(r, [])
            if tok[0] == 'e':
                lst[:] = [t for t in lst if not (t[0] == 'e' and t[1] == tok[1])]
            lst.append(tok)
        for w in writes:
            self.lastw[w] = tok
            self.rd[w] = []

    def op(self, eng, fn, reads=(), writes=(), cost=None):
        self._deps(eng, reads, writes)
        self.cnt[eng] += 1
        seq = self.cnt[eng]
        sem = self.sems[eng]
        self.prog[eng].append(lambda E, fn=fn, sem=sem: fn(E).then_inc(sem, 1))
        self._record(('e', eng, seq), reads, writes)
        self._run_hook()

    def dma(self, eng, out, in_, reads=(), writes=(), is_output=False, cost=None):
        self._deps(eng, reads, writes, is_dma_issue=True)
        if eng == 'pool':
            idx = NDMA_HW + self.dma_next_sw
            self.dma_next_sw = (self.dma_next_sw + 1) % (NDMA - NDMA_HW)
        else:
            idx = self.dma_next
            self.dma_next = (self.dma_next + 1) % NDMA_HW
        self.dma_use[idx] += 1
        n = self.dma_use[idx]
        if n > 1:
            self._wait(eng, ('d', idx, 16 * (n - 1)), 'waw', True)
        sem = self.dma_sems[idx]
        self.prog[eng].append(lambda E, sem=sem, out=out, in_=in_: E.dma_start(out=out, in_=in_).then_inc(sem, 16))
        tok = ('d', idx, 16 * n)
        self._record(tok, reads, writes)
        if is_output:
            self.out_tokens.append(tok)

    def barrier(self):
        for e in ENGS:
            for f in ENGS:
                if f != e and self.cnt[f] > 0:
                    self._wait(e, ('e', f, self.cnt[f]), 'raw', True)
            for idx in range(NDMA):
                if self.dma_use[idx] > 0:
                    self._wait(e, ('d', idx, 16 * self.dma_use[idx]), 'raw', True)

    def finish(self):
        for tok in self.out_tokens:
            self._wait('sp', tok, 'raw', True)
        for e in ENGS:
            if e != 'sp' and self.cnt[e] > 0:
                self._wait('sp', ('e', e, self.cnt[e]), 'raw', True)

    def emit(self):
        nc = self.nc
        prog = self.prog
        with nc.Block() as block:
            @block.tensor
            def _(E):
                for f in prog['pe']:
                    f(E)

            @block.vector
            def _(E):
                for f in prog['dve']:
                    f(E)

            @block.scalar
            def _(E):
                for f in prog['act']:
                    f(E)

            @block.gpsimd
            def _(E):
                for f in prog['pool']:
                    f(E)

            @block.sync
            def _(E):
                for f in prog['sp']:
                    f(E)


DEF_COST = {'pe': 0.4, 'dve': 0.3, 'act': 0.45, 'pool': 0.9, 'sp': 0.1}


def _fs(ap):
    try:
        return int(ap.free_size())
    except Exception:
        return 512


class Deferred:
    def __init__(self, S):
        self.S = S
        self.q = []
        self.pos = 0

    def op(self, eng, fn, reads=(), writes=(), cost=None):
        self.q.append(('op', eng, fn, tuple(reads), tuple(writes), False, DEF_COST[eng] if cost is None else cost))

    def dma(self, eng, out, in_, reads=(), writes=(), is_output=False, cost=None):
        self.q.append(('dma', eng, (out, in_), tuple(reads), tuple(writes), is_output, 2.0 if cost is None else cost))

    def empty(self):
        return self.pos >= len(self.q)

    def peek(self):
        return self.q[self.pos]

    def pop_emit(self):
        it = self.q[self.pos]
        self.pos += 1
        if it[0] == 'op':
            self.S.op(it[1], it[2], it[3], it[4])
        else:
            self.S.dma(it[1], it[2][0], it[2][1], reads=it[3], writes=it[4], is_output=it[5])
        return it


class Sim:
    LAT = 1.6

    def __init__(self):
        self.free = {e: 0.0 for e in ENGS}
        self.wfin = {}
        self.rfin = {}

    def est_start(self, it):
        t = self.free[it[1]]
        for r in it[3]:
            t = max(t, self.wfin.get(r, 0.0) + self.LAT)
        for w in it[4]:
            t = max(t, self.wfin.get(w, 0.0) + self.LAT, self.rfin.get(w, 0.0) + self.LAT)
        return t

    def commit(self, it, start):
        eng = it[1]
        if it[0] == 'dma':
            self.free[eng] = start + 0.06
            fin = start + it[6]
        else:
            fin = start + it[6]
            self.free[eng] = fin
        for r in it[3]:
            if self.rfin.get(r, 0.0) < fin:
                self.rfin[r] = fin
        for w in it[4]:
            self.wfin[w] = fin
            self.rfin[w] = 0.0


def merge_emit(sim, streams, must_drain):
    while any(streams[i] is not None and not streams[i].empty() for i in must_drain):
        best = None
        for i, q in enumerate(streams):
            if q is None or q.empty():
                continue
            t = sim.est_start(q.peek())
            if best is None or t < best[0] - 1e-9:
                best = (t, i)
        t, i = best
        it = streams[i].pop_emit()
        sim.commit(it, t)


def MM(S, out, pairs, reads, writes, start=True, stop=True):
    cost = sum(max(_fs(r), 64) / 2400.0 + 0.012 for (_, r) in pairs) + 0.15

    def fn(E):
        n = len(pairs)
        ins = None
        for i, (l, r) in enumerate(pairs):
            ins = E.matmul(out, l, r, start=(start and i == 0), stop=(stop and i == n - 1))
        return ins
    S.op('pe', fn, reads, writes, cost=cost)


def TR(S, out, in_, ident, reads, writes):
    S.op('pe', lambda E: E.transpose(out, in_, ident), reads, writes)


def TRN(S, outs_ins, ident, reads, writes):
    def fn(E):
        ins = None
        for o, i in outs_ins:
            ins = E.transpose(o, i, ident)
        return ins
    S.op('pe', fn, reads, writes)


def ACT(S, out, in_, func, reads, writes, bias=None, scale=None, accum_out=None):
    kw = {}
    if bias is not None:
        kw['bias'] = bias
    if scale is not None:
        kw['scale'] = scale
    if accum_out is not None:
        kw['accum_out'] = accum_out
    S.op('act', lambda E: E.activation(out, in_, func, **kw), reads, writes,
         cost=(230 + _fs(out)) / 1400.0 + (0.0 if func in (AF.Identity, AF.Square) else 0.7))


def _ecost(eng, out):
    n = _fs(out)
    if eng == 'pool':
        return (120 + 2.5 * n) / 1200.0
    return (70 + n) / 960.0


def TT(S, eng, out, a, b, op, reads, writes):
    S.op(eng, lambda E: E.tensor_tensor(out, a, b, op), reads, writes, cost=_ecost(eng, out))


def TS(S, eng, out, a, s1, s2, op0, op1, reads, writes):
    if op1 is None:
        S.op(eng, lambda E: E.tensor_scalar(out, a, s1, s2, op0), reads, writes, cost=_ecost(eng, out))
    else:
        S.op(eng, lambda E: E.tensor_scalar(out, a, s1, s2, op0, op1), reads, writes, cost=_ecost(eng, out))


def STT(S, eng, out, in0, scalar, in1, op0, op1, reads, writes):
    S.op(eng, lambda E: E.scalar_tensor_tensor(out, in0, scalar, in1, op0, op1), reads, writes, cost=_ecost(eng, out))


def CP(S, eng, out, in_, reads, writes):
    if eng == 'act':
        S.op('act', lambda E: E.activation(out, in_, AF.Identity), reads, writes, cost=(230 + _fs(out)) / 1400.0)
    else:
        S.op(eng, lambda E: E.tensor_copy(out, in_), reads, writes, cost=_ecost(eng, out))


class Rot:
    def __init__(self, items):
        self.items = items
        self.i = 0

    def next(self):
        it = self.items[self.i % len(self.items)]
        self.i += 1
        return it


def build_program():
    nc = bass.Bass("TRN2", target_bir_lowering=False)
    dt_in = lambda name, shape: nc.dram_tensor(name, shape, F32, kind="ExternalInput").ap()
    x_tok_d = dt_in("x_tok", [S_LEN, D])
    xT_d = dt_in("xT", [D, S_LEN])
    wfm_d = dt_in("w_fm", [D, 1664])
    wv_d = dt_in("w_v", [D, 128])
    bfm_d = dt_in("b_fm", [128, 13])
    rows_d = dt_in("rows", [1, 128 + 1024])
    cos_d = dt_in("cos_t", [128, S_LEN])
    sin_d = dt_in("sin_t", [128, S_LEN])
    consts_d = dt_in("consts", [128, 128 * 5])
    wbd_d = dt_in("w_bd", [128, 8 * 128])
    rnnp_d = dt_in("rnnp", [128, 4 * 9])
    bc_d = dt_in("bcast", [1, 512 + 4 * 1024 + 8])
    wout_d = dt_in("w_out", [D, D])
    wq_d = dt_in("w_q", [D, 2048])
    keys_d = dt_in("keysT", [128, 256])
    u_d = dt_in("u_l", [16384, 1024])
    v_d = dt_in("v_l", [16384, 1024])
    out_d = nc.dram_tensor("out", [S_LEN, D], F32, kind="ExternalOutput").ap()
    dbg = {}
    if DEBUG:
        dbg['h1'] = nc.dram_tensor("dbg_h1", [S_LEN, D], F32, kind="ExternalOutput").ap()
        dbg['route'] = nc.dram_tensor("dbg_route", [128, 3 * S_LEN], BF16, kind="ExternalOutput").ap()

    h1_d = nc.dram_tensor("h1_scr", [S_LEN, D], F32).ap()
    h1T_d = nc.dram_tensor("h1T_scr", [D, S_LEN], BF16).ap()
    ubf_d = nc.dram_tensor("u_bf", [16384, 1024], BF16).ap()
    vbf_d = nc.dram_tensor("v_bf", [16384, 1024], BF16).ap()

    with ExitStack() as es:
        S = Sched(nc, es)

        def T(stack, name, shape, dt=F32):
            return stack.enter_context(nc.sbuf_tensor("sb_" + name, shape, dt))

        consts_f = T(es, "consts_f", [128, 640])
        consts_b = T(es, "consts_b", [128, 640], BF16)
        S.dma('sp', consts_f[:], consts_d, writes=['consts_f'])
        S.dma('pool', consts_b[:], consts_d, writes=['consts_b'])
        perm_b = consts_b[:, 0:128]
        maskC_b = consts_b[:, 128:256]
        maskP_b = consts_b[:, 256:384]
        ident_b = consts_b[:, 384:512]
        iota_b = consts_b[:, 512:640]
        ident_f = consts_f[:, 384:512]
        iota16_f = consts_f[:, 512:528]
        ones_f = T(es, "ones_f", [128, 128])
        S.op('dve', lambda E: E.memset(ones_f[:], 1.0), writes=['ones_f'])
        ones_b = T(es, "ones_b", [128, 128], BF16)
        S.op('dve', lambda E: E.memset(ones_b[:], 1.0), writes=['ones_b'])
        esink = T(es, "esink", [128, 8])
        S.dma('sp', esink[:], bc_d[:, 4608:4616].partition_broadcast(128), writes=['esink'])
        ACT(S, esink[:], esink[:], AF.Exp, ['esink'], ['esink'])
        route_d = nc.dram_tensor("route_scr", [128, 3, S_LEN], BF16).ap()
        wqb_d = nc.dram_tensor("wq_bf", [D, 2048], BF16).ap()

        NCH = 32
        rows_per = 16384 // NCH
        cast_jobs = []
        for i in range(NCH):
            sl = slice(i * rows_per, (i + 1) * rows_per)
            cast_jobs.append((ubf_d[sl, :], u_d[sl, :], ('ubf', i)))
            cast_jobs.append((vbf_d[sl, :], v_d[sl, :], ('vbf', i)))

        def issue_casts(k):
            for _ in range(k):
                if cast_jobs:
                    o_, i_, key = cast_jobs.pop(0)
                    S.dma('pool', o_, i_, writes=[key])

        with ExitStack() as p1:
            banks = [p1.enter_context(nc.psum_tensor("p1bank%d" % i, [128, 512], F32)) for i in range(7)]
            bankb = p1.enter_context(nc.psum_tensor("p1bankb", [128, 1024], BF16))
            wfm = T(p1, "wfm", [128, 8, 1664], BF16)
            wv = T(p1, "wv", [128, 8, 128], BF16)
            wout = T(p1, "wout", [128, 8, 1024], BF16)
            for kc in range(8):
                S.dma('pool', wfm[:, kc, :], wfm_d[kc * 128:(kc + 1) * 128, :], writes=['wfm'])
            S.dma('pool', wv[:], wv_d.rearrange("(c p) n -> p c n", p=128), writes=['wv'])
            for kc in range(8):
                S.dma('pool', wout[:, kc, :], wout_d[kc * 128:(kc + 1) * 128, :], writes=['wout'])
            bc = T(p1, "bc", [128, 2560])
            S.dma('sp', bc[:], bc_d[:, 0:2560].partition_broadcast(128), writes=['bc'])
            ga_bc = bc[:, 0:512]
            ln1g = bc[:, 512:1536]
            ln1b = bc[:, 1536:2560]
            keysT = T(p1, "keysT", [128, 256], BF16)
            S.dma('pool', keysT[:], keys_d, writes=['keysT'])
            wqs = [T(p1, "wqs%d" % i, [128, 8, 256], BF16) for i in range(2)]
            qpT = T(p1, "qpT", [128, 16, 512], BF16)
            stm = T(p1, "stm", [128, 16, 128])
            vtop = T(p1, "vtop", [128, 16, 16])
            itop = T(p1, "itop", [128, 16, 16], U32)
            itopf = T(p1, "itopf", [128, 16, 16])
            cand = T(p1, "cand", [128, 8, 256])
            sc = T(p1, "sc", [128, 8, 16])
            ci = T(p1, "ci", [128, 8, 16], U32)
            aki = T(p1, "aki", [128, 8, 16], U32)
            bki = T(p1, "bki", [128, 8, 16], U32)
            akf = T(p1, "akf", [128, 8, 16])
            bkf = T(p1, "bkf", [128, 8, 16])
            e1f = T(p1, "e1f", [128, 128])
            e2f = T(p1, "e2f", [128, 128])
            gex = T(p1, "gex", [128, 8, 16])
            gz = T(p1, "gz", [128, 8])
            gf = T(p1, "gf", [128, 128])
            rst = T(p1, "rst", [128, 3, 128], BF16)
            bfm = T(p1, "bfm", [128, 13])
            S.dma('sp', bfm[:], bfm_d, writes=['bfm'])
            for kc in range(8):
                S.dma('pool', wqb_d[kc * 128:(kc + 1) * 128, :], wq_d[kc * 128:(kc + 1) * 128, :], writes=['wqb'])
            rows_b = T(p1, "rows_b", [1, 1152], BF16)
            S.dma('pool', rows_b[:], rows_d, writes=['rows_b'])
            wbd = T(p1, "wbd", [128, 8, 128], BF16)
            S.dma('pool', wbd[:], wbd_d.rearrange("p (j n) -> p j n", j=8), writes=['wbd'])
            rnnp = T(p1, "rnnp", [128, 4, 9])
            S.dma('sp', rnnp[:], rnnp_d.rearrange("p (j n) -> p j n", j=4), writes=['rnnp'])
            cL = T(p1, "cL", [128, 4])
            ACT(S, cL[:], rnnp[:, :, 7], AF.Exp, ['rnnp'], ['cL'], scale=-1.0)
            ACT(S, cL[:], cL[:], AF.Ln, ['cL'], ['cL'], bias=1.0)
            TS(S, 'dve', cL[:], cL[:], -8.0, None, ALU.mult, None, ['cL'], ['cL'])
            cLh = T(p1, "cLh", [128, 4])
            TS(S, 'dve', cLh[:], cL[:], 0.5, None, ALU.mult, None, ['cL'], ['cLh'])
            hb = T(p1, "hb", [128, 4, 2])
            TS(S, 'dve', hb[:], rnnp[:, :, 5:7], 0.5, None, ALU.mult, None, ['rnnp'], ['hb'])

            xTb = [T(p1, "xTb%d" % i, [128, 8, 512], BF16) for i in range(1)]
            xtok = [T(p1, "xtok%d" % i, [128, 1024]) for i in range(1)]
            cs = [T(p1, "cs%d" % i, [128, 2, 512]) for i in range(1)]
            zq = T(p1, "zq", [128, 512])
            zqb = T(p1, "zqb", [128, 512], BF16)
            t1 = T(p1, "t1", [128, 512])
            t2 = T(p1, "t2", [128, 512])
            qrot = T(p1, "qrot", [128, 5, 512], BF16)
            kprev = T(p1, "kprev", [128, 128], BF16)
            vaug = T(p1, "vaug", [128, 5, 2, 65], BF16)
            S.op('dve', lambda E: E.memset(vaug[:], 1.0), writes=['vaug%d' % i for i in range(5)])
            ex = [T(p1, "ex%d" % i, [128, 2, 512], BF16) for i in range(2)]
            den = T(p1, "den", [128, 8])
            attn = T(p1, "attn", [128, 512])
            st1 = T(p1, "st1", [128, 4])
            attn_n = T(p1, "attn_n", [128, 512], BF16)
            mixT = T(p1, "mixT", [128, 8, 512], BF16)
            xr = T(p1, "xr", [128, 4, 515])
            S.op('dve', lambda E: E.memset(xr[:], 0.0), writes=['xr%d' % j for j in range(4)])
            gl = T(p1, "gl", [128, 512])
            xc = T(p1, "xc", [128, 512])
            rr = T(p1, "rr", [128, 512])
            ii = T(p1, "ii", [128, 512])
            bb = T(p1, "bb", [128, 512])
            hh = T(p1, "hh", [128, 512])
            hstate = T(p1, "hstate", [128, 4])
            ro = T(p1, "ro", [128, 4, 512], BF16)
            y = T(p1, "y", [128, 1024])
            bst = T(p1, "bst", [128, 2, 6])
            mv = T(p1, "mv", [128, 2])
            rstd = T(p1, "rstd", [128, 1])
            h1 = [T(p1, "h1_%d" % i, [128, 1024]) for i in range(1)]
            h1b = T(p1, "h1b", [128, 1024], BF16)
            h1Tg = [T(p1, "h1Tg%d" % i, [128, 8, 512], BF16) for i in range(2)]

            pool = Rot([(banks[i], 'bank%d' % i) for i in range(4)])
            rpool = Rot([(banks[i], 'bank%d' % i) for i in (4, 5)])
            ssq_bank, ssq_key = banks[6], 'bank6'


            def emit_rpre(R, g, h1T_g, kht):
                for piece in range(8):
                    wq_t = wqs[piece % 2]
                    kwq = 'wqs%d' % (piece % 2)
                    R.dma('act', wq_t[:], wqb_d.rearrange("(c p) n -> p c n", p=128)[:, :, piece * 256:(piece + 1) * 256],
                          reads=['wqb'], writes=[kwq])
                    for i in range(2):
                        cc = piece * 2 + i
                        pq, kq = rpool.next()
                        MM(R, pq[:], [(wq_t[:, kc, i * 128:(i + 1) * 128], h1T_g[:, kc, :]) for kc in range(8)],
                           [kwq, kht], [kq])
                        CP(R, 'act', qpT[:, cc, :], pq[:], [kq], [('r_qpT', cc)])

            def emit_routing(R, g, s_, h1T_g, kht):
                qs = slice(s_ * 128, (s_ + 1) * 128)
                blk = g * 4 + s_
                tok = slice(blk * 128, (blk + 1) * 128)
                for q4 in range(4):
                    pb, kb_ = rpool.next()
                    pbv = pb[:].rearrange("p (c e) -> p c e", c=4)

                    def scfn(E, pbv=pbv, q4=q4, qs=qs):
                        ins = None
                        for i in range(4):
                            cc = q4 * 4 + i
                            half = cc % 2
                            ins = E.matmul(pbv[:, i, :], qpT[:, cc, qs], keysT[:, half * 128:(half + 1) * 128],
                                           start=True, stop=True)
                        return ins
                    R.op('pe', scfn, [('r_qpT', q4 * 4 + i) for i in range(4)] + ['keysT'], [kb_])
                    CP(R, 'act', stm[:, q4 * 4:(q4 + 1) * 4, :], pbv, [kb_], [('r_stm', q4 * 4 + i) for i in range(4)])
                for blk4 in range(4):
                    ccs = range(blk4 * 4, blk4 * 4 + 4)
                    for cc in ccs:
                        R.op('dve', lambda E, cc=cc: E.max(vtop[:, cc, 0:8], stm[:, cc, :]), [('r_stm', cc)], [('r_vtop', cc)])
                    for cc in ccs:
                        R.op('dve', lambda E, cc=cc: E.max_index(itop[:, cc, 0:8], vtop[:, cc, 0:8], stm[:, cc, :]),
                             [('r_stm', cc), ('r_vtop', cc)], [('r_itop', cc)])
                    for cc in ccs:
                        R.op('dve', lambda E, cc=cc: E.match_replace(stm[:, cc, :], vtop[:, cc, 0:8], stm[:, cc, :], -1e30),
                             [('r_stm', cc), ('r_vtop', cc)], [('r_stm', cc)])
                    for cc in ccs:
                        R.op('dve', lambda E, cc=cc: E.max(vtop[:, cc, 8:16], stm[:, cc, :]), [('r_stm', cc)], [('r_vtop', cc)])
                    for cc in ccs:
                        R.op('dve', lambda E, cc=cc: E.max_index(itop[:, cc, 8:16], vtop[:, cc, 8:16], stm[:, cc, :]),
                             [('r_stm', cc), ('r_vtop', cc)], [('r_itop', cc)])
                allv = [('r_vtop', cc) for cc in range(16)]
                alli = [('r_itop', cc) for cc in range(16)]
                allstm = [('r_stm', cc) for cc in range(16)]
                CP(R, 'dve', itopf[:], itop[:], alli, ['r_itopf'])
                v4 = vtop[:].rearrange("p (h f) a -> p h f a", f=2)
                i4 = itopf[:].rearrange("p (h f) a -> p h f a", f=2)
                TT(R, 'dve', cand[:].rearrange("p h (a b) -> p h a b", a=16),
                   v4[:, :, 0, :].unsqueeze(3).to_broadcast([128, 8, 16, 16]),
                   v4[:, :, 1, :].unsqueeze(2).to_broadcast([128, 8, 16, 16]), ALU.add, allv, ['r_cand'])
                for blk2 in range(2):
                    hs = range(blk2 * 4, blk2 * 4 + 4)
                    for h in hs:
                        R.op('dve', lambda E, h=h: E.max(sc[:, h, 0:8], cand[:, h, :]), ['r_cand'], [('r_sc', h)])
                    for h in hs:
                        R.op('dve', lambda E, h=h: E.max_index(ci[:, h, 0:8], sc[:, h, 0:8], cand[:, h, :]),
                             ['r_cand', ('r_sc', h)], [('r_ci', h)])
                    for h in hs:
                        R.op('dve', lambda E, h=h: E.match_replace(cand[:, h, :], sc[:, h, 0:8], cand[:, h, :], -1e30),
                             ['r_cand', ('r_sc', h), ('r_ci', h)], [('r_cw', h)])
                    for h in hs:
                        R.op('dve', lambda E, h=h: E.max(sc[:, h, 8:16], cand[:, h, :]), [('r_cw', h)], [('r_sc', h)])
                    for h in hs:
                        R.op('dve', lambda E, h=h: E.max_index(ci[:, h, 8:16], sc[:, h, 8:16], cand[:, h, :]),
                             [('r_cw', h), ('r_sc', h)], [('r_ci', h)])
                allsc = [('r_sc', h) for h in range(8)]
                allci = [('r_ci', h) for h in range(8)]
                allcw = [('r_cw', h) for h in range(8)]
                R.op('dve', lambda E: E.tensor_single_scalar(aki[:], ci[:], 4, ALU.logical_shift_right), allci, ['r_aki'])
                R.op('dve', lambda E: E.tensor_single_scalar(bki[:], ci[:], 15, ALU.bitwise_and), allci, ['r_bki'])
                CP(R, 'dve', akf[:], aki[:], ['r_aki'], ['r_akf'])
                CP(R, 'dve', bkf[:], bki[:], ['r_bki'], ['r_bkf'])
                TT(R, 'dve', gex[:], sc[:], sc[:, :, 0:1].to_broadcast([128, 8, 16]), ALU.subtract, allsc, ['r_gex'])
                ACT(R, gex[:], gex[:], AF.Exp, ['r_gex'], ['r_gex'])
                R.op('dve', lambda E: E.tensor_reduce(gz[:], gex[:], AX.X, ALU.add), ['r_gex'], ['r_gz'])
                R.op('dve', lambda E: E.reciprocal(gz[:], gz[:]), ['r_gz'], ['r_gz'])
                TT(R, 'dve', gf[:].rearrange("p (h k) -> p h k", h=8), gex[:],
                   gz[:].unsqueeze(2).to_broadcast([128, 8, 16]), ALU.mult, ['r_gex', 'r_gz'], ['r_gf'])
                eqv = stm[:].rearrange("p c e -> p (c e)").rearrange("p (h k a) -> p h k a", h=8, k=16)
                eq2v = cand[:].rearrange("p h (k a) -> p h k a", k=16)
                iob = iota16_f.unsqueeze(1).unsqueeze(1).to_broadcast([128, 8, 16, 16])
                TT(R, 'dve', eqv, akf[:].unsqueeze(3).to_broadcast([128, 8, 16, 16]), iob, ALU.is_equal,
                   ['r_akf', 'consts_f'] + allstm + alli, allstm)
                TT(R, 'dve', eqv, eqv, i4[:, :, 0, :].unsqueeze(2).to_broadcast([128, 8, 16, 16]), ALU.mult,
                   allstm + ['r_itopf'], allstm)
                R.op('dve', lambda E: E.tensor_reduce(e1f[:].rearrange("p (h k) -> p h k", h=8), eqv, AX.X, ALU.add),
                     allstm, ['r_e1f'])
                TT(R, 'dve', eq2v, bkf[:].unsqueeze(3).to_broadcast([128, 8, 16, 16]), iob, ALU.is_equal,
                   ['r_bkf', 'consts_f', 'r_cand'] + allcw + allci, ['r_cand'] + allcw)
                TT(R, 'dve', eq2v, eq2v, i4[:, :, 1, :].unsqueeze(2).to_broadcast([128, 8, 16, 16]), ALU.mult,
                   ['r_cand', 'r_itopf'] + allcw, ['r_cand'] + allcw)
                R.op('dve', lambda E: E.tensor_reduce(e2f[:].rearrange("p (h k) -> p h k", h=8), eq2v, AX.X, ALU.add),
                     ['r_cand'] + allcw, ['r_e2f'])
                for i3, (src, ksrc) in enumerate([(e1f, 'r_e1f'), (e2f, 'r_e2f'), (gf, 'r_gf')]):
                    ptr, ktr = rpool.next()
                    TR(R, ptr[:, 0:128], src[:], ident_f, [ksrc, 'consts_f'], [ktr])
                    CP(R, 'act', rst[:, i3, :], ptr[:, 0:128], [ktr], [('r_rst', i3)])
                R.dma('sp', route_d[:, :, tok], rst[:], reads=[('r_rst', i) for i in range(3)], writes=[('route', blk)])

            NG = S_LEN // 512

            def load_x(g_):
                t0_ = g_ * 512
                for kc in range(8):
                    S.dma('pool', xTb[0][:, kc, :], xT_d[kc * 128:(kc + 1) * 128, t0_:t0_ + 512],
                          writes=['xTb0'])

            def load_cs(g_):
                t0_ = g_ * 512
                S.dma('sp', cs[0][:, 0, :], cos_d[:, t0_:t0_ + 512], writes=['cs0'])
                S.dma('sp', cs[0][:, 1, :], sin_d[:, t0_:t0_ + 512], writes=['cs0'])

            S_real = S
            sim = Sim()
            QR_prev = None
            poolA = Rot([(banks[i], 'bank%d' % i) for i in (0, 1)])
            poolB = Rot([(banks[i], 'bank%d' % i) for i in (2, 3)])
            poolC = Rot([(banks[i], 'bank%d' % i) for i in range(4)])
            for g in range(NG):
                t0 = g * 512
                xT_g = xTb[0]
                kx = 'xTb0'
                cs_g = cs[0]
                kcs = 'cs0'
                if g == 0:
                    load_x(0)
                    load_cs(0)
                issue_casts(8)
                QA, QB, QC, QR = Deferred(S_real), Deferred(S_real), Deferred(S_real), Deferred(S_real)
                S = QA
                pool = poolA

                for c in range(5):
                    pz, kz = pool.next()
                    MM(S, pz[:], [(wfm[:, kc, c * 128:(c + 1) * 128], xT_g[:, kc, :]) for kc in range(8)],
                       ['wfm', kx], [kz])
                    ACT(S, zq[:], pz[:], AF.Identity, [kz, 'bfm'], ['zq'], bias=bfm[:, c:c + 1])
                    CP(S, 'dve', attn_n[:], zq[:], ['zq'], ['attn_n'])
                    pp, kp = pool.next()
                    MM(S, pp[:], [(perm_b, attn_n[:])], ['consts_b', 'attn_n'], [kp])
                    TT(S, 'dve', t1[:], zq[:], cs_g[:, 0, :], ALU.mult, ['zq', kcs], ['t1'])
                    TT(S, 'dve', t2[:], pp[:], cs_g[:, 1, :], ALU.mult, [kp, kcs], ['t2'])
                    TT(S, 'dve', qrot[:, c, :], t1[:], t2[:], ALU.add, ['t1', 't2'], [('qrot', c)])

                if g + 1 < NG:
                    load_cs(g + 1)
                for s in range(4):
                    pv, kv = pool.next()
                    prs = [(xT_g[:, kc, s * 128:(s + 1) * 128], wv[:, kc, :]) for kc in range(8)]
                    prs.append((ones_b[0:1, 0:128], rows_b[0:1, 0:128]))
                    MM(S, pv[:, 0:128], prs, [kx, 'wv', 'ones_b', 'rows_b'], [kv])
                    CP(S, 'act', vaug[:, 1 + s, :, 0:64], pv[:, 0:128].rearrange("p (h d) -> p h d", h=2),
                       [kv], ['vaug%d' % (1 + s)])

                def attn_block(s):
                    n = 4 * g + s
                    qs = slice(s * 128, (s + 1) * 128)
                    for hk in range(2):
                        ps_ = slice(hk * 64, hk * 64 + 64)
                        e_t = ex[hk]
                        ke = 'ex%d' % hk
                        kbs = [1] if n == 0 else [0, 1]
                        for kb in kbs:
                            if kb == 1:
                                kT = qrot[ps_, 4, qs]
                                kread = ('qrot', 4)
                            elif s == 0:
                                kT = kprev[ps_, :]
                                kread = 'kprev'
                            else:
                                kT = qrot[ps_, 4, (s - 1) * 128:s * 128]
                                kread = ('qrot', 4)
                            psc, ksc = pool.next()
                            MM(S, psc[:].rearrange("p (g q) -> p g q", g=4), [(kT, qrot[ps_, 0:4, qs])],
                               [kread] + [('qrot', c) for c in range(4)], [ksc])
                            ACT(S, e_t[:, kb, :], psc[:], AF.Exp, [ksc], [(ke, kb)], scale=0.125)
                            mk = maskC_b if kb == 1 else maskP_b
                            TT(S, 'dve', e_t[:, kb, :].rearrange("p (g q) -> p g q", g=4),
                               e_t[:, kb, :].rearrange("p (g q) -> p g q", g=4),
                               mk.unsqueeze(1).to_broadcast([128, 4, 128]), ALU.mult,
                               [(ke, kb), 'consts_b'], [(ke, kb)])
                        po_, ko = pool.next()
                        pov = po_[:, 0:260].rearrange("p (g d) -> p g d", g=4)

                        def pvfn(E, pov=pov, e_t=e_t, kbs=kbs, s=s, hk=hk):
                            ins = None
                            for gi in range(4):
                                for j, kb in enumerate(kbs):
                                    slot = s + kb
                                    ins = E.matmul(pov[:, gi, :], e_t[:, kb, gi * 128:(gi + 1) * 128],
                                                   vaug[:, slot, hk, :], start=(j == 0), stop=(j == len(kbs) - 1))
                            return ins
                        S.op('pe', pvfn, [(ke, kb) for kb in kbs] + ['vaug%d' % (s + kb) for kb in kbs], [ko])
                        TT(S, 'dve', den[:, hk * 4:(hk + 1) * 4], pov[:, :, 64], esink[:, hk * 4:(hk + 1) * 4], ALU.add,
                           [ko, 'esink'], [('den', hk)])
                        S.op('dve', lambda E, hk=hk: E.reciprocal(den[:, hk * 4:(hk + 1) * 4], den[:, hk * 4:(hk + 1) * 4]),
                             [('den', hk)], [('den', hk)])
                        TT(S, 'dve', attn[:, hk * 256:(hk + 1) * 256].rearrange("p (g d) -> p g d", g=4),
                           pov[:, :, 0:64], den[:, hk * 4:(hk + 1) * 4].unsqueeze(2).to_broadcast([128, 4, 64]),
                           ALU.mult, [ko, ('den', hk)], [('attn', hk)])
                    ACT(S, y[:, 0:512], attn[:], AF.Square, [('attn', 0), ('attn', 1)], [('y', 0), 'st1a'],
                        accum_out=st1[:, 0:1])
                    TS(S, 'dve', st1[:, 1:2], st1[:, 0:1], 1.0 / 512, EPS, ALU.mult, ALU.add, ['st1a'], ['st1b'])
                    ACT(S, st1[:, 2:3], st1[:, 1:2], AF.Sqrt, ['st1b'], ['st1c'])
                    S.op('dve', lambda E: E.reciprocal(st1[:, 3:4], st1[:, 2:3]), ['st1c'], ['st1d'])
                    STT(S, 'dve', attn_n[:], attn[:], st1[:, 3:4], ga_bc, ALU.mult, ALU.mult,
                        [('attn', 0), ('attn', 1), 'st1d', 'bc'], ['attn_n'])
                    ptv = bankb[:, 0:512].rearrange("p (j q) -> p j q", j=4)
                    TRN(S, [(ptv[:, j, :], attn_n[:, j * 128:(j + 1) * 128]) for j in range(4)], ident_b,
                        ['attn_n', 'consts_b'], ['bankb'])
                    CP(S, 'act', mixT[:, 0:4, qs], ptv, ['bankb'], [('mixT', s)])

                def rnn_chunk(j):
                    pz, kz = pool.next()
                    MM(S, pz[:], [(wfm[:, kc, (5 + j) * 128:(6 + j) * 128], xT_g[:, kc, :]) for kc in range(8)],
                       ['wfm', kx], [kz])
                    kxr = 'xr%d' % j
                    ACT(S, xr[:, j, 3:515], pz[:], AF.Identity, [kz, 'bfm'], [kxr], bias=bfm[:, 5 + j:6 + j])
                    pz2, kz2 = pool.next()
                    MM(S, pz2[:], [(wfm[:, kc, (9 + j) * 128:(10 + j) * 128], xT_g[:, kc, :]) for kc in range(8)],
                       ['wfm', kx], [kz2])
                    ACT(S, gl[:], pz2[:], AF.Gelu_apprx_tanh, [kz2, 'bfm'], ['gl'], bias=bfm[:, 9 + j:10 + j])
                    TS(S, 'dve', xc[:], xr[:, j, 0:512], rnnp[:, j, 0:1], rnnp[:, j, 4:5], ALU.mult, ALU.add,
                       [kxr, 'rnnp'], ['xc'])
                    for k in range(1, 4):
                        STT(S, 'dve', xc[:], xr[:, j, k:k + 512], rnnp[:, j, k:k + 1], xc[:], ALU.mult, ALU.add,
                            [kxr, 'rnnp', 'xc'], ['xc'])
                    CP(S, 'dve', xr[:, j, 0:3], xr[:, j, 512:515], [kxr], [kxr])
                    CP(S, 'dve', zqb[:], xc[:], ['xc'], ['zqb'])
                    pa, ka = pool.next()
                    MM(S, pa[:], [(wbd[:, j, :], zqb[:])], ['wbd', 'zqb'], [ka])
                    pxx, kxx = pool.next()
                    MM(S, pxx[:], [(wbd[:, 4 + j, :], zqb[:])], ['wbd', 'zqb'], [kxx])
                    ACT(S, rr[:], pa[:], AF.Tanh, [ka, 'hb'], ['rr'], bias=hb[:, j, 0:1], scale=0.5)
                    ACT(S, ii[:], pxx[:], AF.Tanh, [kxx, 'hb'], ['ii'], bias=hb[:, j, 1:2], scale=0.5)
                    ACT(S, rr[:], rr[:], AF.Exp, ['rr', 'cLh'], ['rr'], scale=cLh[:, j:j + 1], bias=cLh[:, j:j + 1])
                    TT(S, 'dve', bb[:], rr[:], rr[:], ALU.mult, ['rr'], ['bb'])
                    ACT(S, bb[:], bb[:], AF.Sqrt, ['bb'], ['bb'], bias=0.25, scale=-0.25)
                    STT(S, 'dve', ii[:], ii[:], 1.0, xc[:], ALU.add, ALU.mult, ['ii', 'xc'], ['ii'])
                    TT(S, 'dve', bb[:], bb[:], ii[:], ALU.mult, ['bb', 'ii'], ['bb'])
                    init = 0.0 if g == 0 else hstate[:, j:j + 1]
                    S.op('dve', lambda E, init=init: E.tensor_tensor_scan(hh[:], rr[:], bb[:], init, ALU.mult, ALU.add),
                         ['rr', 'bb', ('hstate', j)], ['hh'])
                    CP(S, 'dve', hstate[:, j:j + 1], hh[:, 511:512], ['hh'], [('hstate', j)])
                    TT(S, 'dve', ro[:, j, :], gl[:], hh[:], ALU.mult, ['gl', 'hh'], [('ro', j)])
                    ACT(S, hh[:], ro[:, j, :], AF.Square, [('ro', j)], ['hh'])
                    MM(S, ssq_bank[:], [(ones_f[:], hh[:])], ['ones_f', 'hh'], [ssq_key], start=(j == 0), stop=(j == 3))
                for sj in range(4):
                    attn_block(sj)
                CP(S, 'pool', kprev[:], qrot[:, 4, 384:512], [('qrot', 4)], ['kprev'])
                CP(S, 'pool', vaug[:, 0, :, :], vaug[:, 4, :, :], ['vaug4'], ['vaug0'])

                S = QB
                pool = poolB
                for sj in range(4):
                    rnn_chunk(sj)
                TS(S, 'dve', gl[:], ssq_bank[:], 1.0 / 512, EPS, ALU.mult, ALU.add, [ssq_key], ['gl'])
                ACT(S, gl[:], gl[:], AF.Sqrt, ['gl'], ['gl'])
                S.op('dve', lambda E: E.reciprocal(gl[:], gl[:]), ['gl'], ['gl'])
                for j in range(4):
                    STT(S, 'dve', mixT[:, 4 + j, :], ro[:, j, :], rnnp[:, j, 8:9], gl[:], ALU.mult, ALU.mult,
                        [('ro', j), 'rnnp', 'gl'], [('mixTr', j)])

                S = QC
                pool = poolC
                h1T_g = h1Tg[g % 2]
                kht = 'h1Tg%d' % (g % 2)
                for s in range(4):
                    qs = slice(s * 128, (s + 1) * 128)
                    xt = xtok[0]
                    kxt = 'xtok0'
                    S.dma('sp', xt[:], x_tok_d[t0 + s * 128:t0 + (s + 1) * 128, :], writes=[kxt])
                    for half in range(2):
                        py, ky = pool.next()
                        cs_ = slice(half * 512, (half + 1) * 512)
                        prs = [(mixT[:, kc, qs], wout[:, kc, cs_]) for kc in range(8)]
                        prs.append((ones_b[0:1, 0:128], rows_b[0:1, 128 + half * 512:128 + (half + 1) * 512]))
                        MM(S, py[:], prs, [('mixT', s)] + [('mixTr', j) for j in range(4)] + ['wout', 'ones_b', 'rows_b'], [ky])
                        STT(S, 'dve', y[:, cs_], xt[:, cs_], ALPHA, py[:], ALU.mult, ALU.add, [kxt, ky], [('y', half)])
                        S.op('dve', lambda E, half=half, cs_=cs_: E.bn_stats(bst[:, half, :], y[:, cs_]),
                             [('y', half)], [('bst', half)])
                    S.op('dve', lambda E: E.bn_aggr(mv[:], bst[:].rearrange("p a b -> p (a b)")),
                         [('bst', 0), ('bst', 1)], ['mv'])
                    TS(S, 'dve', rstd[:], mv[:, 1:2], EPS, None, ALU.add, None, ['mv'], ['rstd'])
                    ACT(S, rstd[:], rstd[:], AF.Sqrt, ['rstd'], ['rstd'])
                    S.op('dve', lambda E: E.reciprocal(rstd[:], rstd[:]), ['rstd'], ['rstd'])
                    h1t = h1[0]
                    kh = 'h1_0'
                    STT(S, 'dve', h1t[:], y[:], mv[:, 0:1], ln1g, ALU.subtract, ALU.mult,
                        [('y', 0), ('y', 1), 'mv', 'bc'], [kh])
                    STT(S, 'dve', h1t[:], h1t[:], rstd[:], ln1b, ALU.mult, ALU.add, [kh, 'rstd', 'bc'], [kh])
                    S.dma('sp', h1_d[t0 + s * 128:t0 + (s + 1) * 128, :], h1t[:], reads=[kh], writes=[('h1d', n_blk(g, s))])
                    if DEBUG:
                        S.dma('sp', dbg['h1'][t0 + s * 128:t0 + (s + 1) * 128, :], h1t[:], reads=[kh], is_output=True)
                    CP(S, 'act', h1b[:], h1t[:], [kh], ['h1b'])
                    pt8 = bankb[:].rearrange("p (j q) -> p j q", j=8)
                    TRN(S, [(pt8[:, j, :], h1b[:, j * 128:(j + 1) * 128]) for j in range(8)], ident_b,
                        ['h1b', 'consts_b'], ['bankb'])
                    CP(S, 'act', h1T_g[:, :, qs], pt8, ['bankb'], [kht])
                S.dma('sp', h1T_d.rearrange("(c p) t -> p c t", p=128)[:, :, t0:t0 + 512], h1T_g[:], reads=[kht],
                      writes=[('h1Td', g)])
                emit_rpre(QR, g, h1T_g, kht)
                for s_r in range(4):
                    emit_routing(QR, g, s_r, h1T_g, kht)
                S = S_real
                merge_emit(sim, [QR_prev, QA, QB], [1, 2])
                if g + 1 < NG:
                    load_x(g + 1)
                merge_emit(sim, [QR_prev, QC], [0, 1])
                QR_prev = QR
            merge_emit(sim, [QR_prev], [0])
            pool = None
            if DEBUG:
                S.dma('sp', dbg['route'].rearrange("p (a t) -> p a t", a=3), route_d,
                      reads=[('route', b_) for b_ in range(32)], is_output=True)

        issue_casts(1000)
        S.barrier()
        with ExitStack() as p3:
            banks = [p3.enter_context(nc.psum_tensor("p3bank%d" % i, [128, 512], F32)) for i in range(8)]
            TT_ = 256
            NT = S_LEN // TT_
            assert NE1 == 128
            HALF = 64
            Gh = [T(p3, "Gh%d" % i, [128, TT_, HALF], BF16) for i in range(2)]
            hTt = [T(p3, "hTt%d" % i, [128, 8, TT_], BF16) for i in range(2)]
            NA = 8
            Abuf = [T(p3, "A%d" % i, [128, HALF], BF16) for i in range(NA)]
            Bbuf = [T(p3, "B%d" % i, [128, 128], BF16) for i in range(NA)]
            EG = 4
            NUB = 3
            Ub = [T(p3, "Ub%d" % i, [128, EG, 1024], BF16) for i in range(NUB)]
            Vb = [T(p3, "Vb%d" % i, [128, EG, 1024], BF16) for i in range(NUB)]
            ge = [T(p3, "ge%d" % i, [128, TT_], BF16) for i in range(3)]
            wT = [T(p3, "wT%d" % i, [128, TT_], BF16) for i in range(4)]
            h1r = [T(p3, "h1r%d" % i, [128, 1024]) for i in range(2)]
            y2 = T(p3, "y2", [128, 1024])
            ys = [T(p3, "ys%d" % i, [128, 1024]) for i in range(2)]
            bc2 = T(p3, "bc2", [128, 2048])
            S.dma('sp', bc2[:], bc_d[:, 2560:4608].partition_broadcast(128), writes=['bc2'])
            ln2g = bc2[:, 0:1024]
            ln2b = bc2[:, 1024:2048]
            rt = [T(p3, "rt%d" % i, [128, 3, TT_], BF16) for i in range(2)]
            bst2 = T(p3, "bst2", [128, 2, 6])
            mv2 = T(p3, "mv2", [128, 2])
            rstd2 = T(p3, "rstd2", [128, 1])
            ob = [T(p3, "ob%d" % i, [128, 1024]) for i in range(2)]

            po = [[(banks[0], 'bank0'), (banks[1], 'bank1')], [(banks[2], 'bank2'), (banks[3], 'bank3')]]
            pa_rot = Rot([(banks[4], 'bank4'), (banks[5], 'bank5')])
            pg_banks = [(banks[6], 'bank6'), (banks[7], 'bank7')]
            ubv = ubf_d.rearrange("(g p) n -> p g n", p=128)
            vbv = vbf_d.rearrange("(g p) n -> p g n", p=128)
            ngrp = NE1 // EG
            NGG = NT * ngrp

            def load_uv(gg):
                b = gg % NUB
                e0 = (gg % ngrp) * EG
                ch = set([(e0 * 128) // rows_per, ((e0 + EG) * 128 - 1) // rows_per])
                S.dma('sp', Ub[b][:], ubv[:, e0:e0 + EG, :], reads=[('ubf', c) for c in ch], writes=['Ub%d' % b])
                S.dma('sp', Vb[b][:], vbv[:, e0:e0 + EG, :], reads=[('vbf', c) for c in ch], writes=['Vb%d' % b])

            def load_rt(tt_):
                t0_ = tt_ * TT_
                S.dma('sp', rt[tt_ % 2][:], route_d[:, :, t0_:t0_ + TT_],
                      reads=[('route', t0_ // 128 + i) for i in range(TT_ // 128)], writes=['rt%d' % (tt_ % 2)])

            def load_hT(tt_):
                t0_ = tt_ * TT_
                S.dma('sp', hTt[tt_ % 2][:], h1T_d.rearrange("(c p) t -> p c t", p=128)[:, :, t0_:t0_ + TT_],
                      reads=[('h1Td', t0_ // 512)], writes=['hTt%d' % (tt_ % 2)])

            bstate = {'n': 0}

            def build_chunk(tt_, hf, chunk):
                bi = (chunk // 2) % 2
                pg, kg = pg_banks[bi]
                pgv = pg[:].rearrange("p (t e) -> p t e", t=8)
                for i4_ in range(4):
                    tl = chunk * 4 + i4_
                    tg = tt_ * TT_ + tl
                    rk = tg // 128
                    ia = bstate['n'] % NA
                    bstate['n'] += 1
                    r_t = rt[tt_ % 2]
                    krt = 'rt%d' % (tt_ % 2)
                    TS(S, 'dve', Abuf[ia][:], iota_b[:, hf * HALF:(hf + 1) * HALF], r_t[:, 0, tl:tl + 1], r_t[:, 2, tl:tl + 1],
                       ALU.is_equal, ALU.mult, ['consts_b', krt], ['A%d' % ia])
                    TS(S, 'dve', Bbuf[ia][:], iota_b, r_t[:, 1, tl:tl + 1], None, ALU.is_equal, None,
                       ['consts_b', krt], ['B%d' % ia])
                    MM(S, pgv[:, (chunk % 2) * 4 + i4_, :], [(Bbuf[ia][:], Abuf[ia][:])], ['A%d' % ia, 'B%d' % ia], [kg])
                if chunk % 2 == 1:
                    c8 = chunk // 2
                    CP(S, 'act', Gh[hf][:, c8 * 8:(c8 + 1) * 8, :], pgv, [kg], ['G%d' % hf])

            for gg in range(min(3, NGG)):
                load_uv(gg)
            load_hT(0)
            load_rt(0)
            for chunk in range(TT_ // 4):
                build_chunk(0, 0, chunk)

            EQ_prev = None
            for tt in range(NT):
                t0 = tt * TT_
                hT_t = hTt[tt % 2]
                kh = 'hTt%d' % (tt % 2)
                if tt + 1 < NT:
                    load_hT(tt + 1)
                    load_rt(tt + 1)

                def u_stage(e1):
                    gg = tt * ngrp + e1 // EG
                    e1l = e1 % EG
                    b = gg % NUB
                    pa, ka = pa_rot.next()
                    MM(S, pa[:, 0:TT_], [(Ub[b][:, e1l, c * 128:(c + 1) * 128], hT_t[:, c, :]) for c in range(8)],
                       ['Ub%d' % b, kh], [ka])
                    g_t = ge[e1 % 3]
                    kge = 'ge%d' % (e1 % 3)
                    ACT(S, g_t[:], pa[:, 0:TT_], AF.Gelu_apprx_tanh, [ka], [kge])
                    w_t = wT[e1 % 4]
                    kw = 'wT%d' % (e1 % 4)
                    hf = e1 // HALF
                    TT(S, 'pool', w_t[:], g_t[:], Gh[hf][:, :, e1 % HALF], ALU.mult, [kge, 'G%d' % hf], [kw])

                def v_stage(e1):
                    gg = tt * ngrp + e1 // EG
                    e1l = e1 % EG
                    b = gg % NUB
                    w_t = wT[e1 % 4]
                    kw = 'wT%d' % (e1 % 4)
                    for sub in range(2):
                        for half in range(2):
                            pb_, kb_ = po[sub][half]
                            MM(S, pb_[:], [(w_t[:, sub * 128:(sub + 1) * 128], Vb[b][:, e1l, half * 512:(half + 1) * 512])],
                               [kw, 'Vb%d' % b], [kb_], start=(e1 == 0), stop=(e1 == NE1 - 1))
                    if e1l == EG - 1 and gg + 3 < NGG:
                        load_uv(gg + 3)

                u_stage(0)
                for e1 in range(NE1):
                    if EQ_prev is not None and e1 >= 2 and e1 % 2 == 0:
                        if not EQ_prev.empty():
                            EQ_prev.pop_emit()
                    if e1 < HALF:
                        build_chunk(tt, 1, e1)
                    elif tt + 1 < NT:
                        build_chunk(tt + 1, 0, e1 - HALF)
                    if e1 + 1 < NE1:
                        u_stage(e1 + 1)
                    v_stage(e1)

                for sub in range(2):
                    for half in range(2):
                        pb_, kb_ = po[sub][half]
                        CP(S, 'act', ys[sub][:, half * 512:(half + 1) * 512], pb_[:], [kb_], [('ys', sub, half)])
                S_real2 = S
                EQ = Deferred(S_real2)
                S = EQ
                for sub in range(2):
                    r0 = t0 + sub * 128
                    h_r = h1r[sub]
                    khr = 'h1r%d' % sub
                    S.dma('sp', h_r[:], h1_d[r0:r0 + 128, :], reads=[('h1d', r0 // 128)], writes=[khr])
                    for half in range(2):
                        cs_ = slice(half * 512, (half + 1) * 512)
                        STT(S, 'dve', y2[:, cs_], h_r[:, cs_], ALPHA, ys[sub][:, cs_], ALU.mult, ALU.add,
                            [khr, ('ys', sub, half)], [('y2', half)])
                        S.op('dve', lambda E, half=half, cs_=cs_: E.bn_stats(bst2[:, half, :], y2[:, cs_]),
                             [('y2', half)], [('bst2', half)])
                    S.op('dve', lambda E: E.bn_aggr(mv2[:], bst2[:].rearrange("p a b -> p (a b)")),
                         [('bst2', 0), ('bst2', 1)], ['mv2'])
                    TS(S, 'dve', rstd2[:], mv2[:, 1:2], EPS, None, ALU.add, None, ['mv2'], ['rstd2'])
                    ACT(S, rstd2[:], rstd2[:], AF.Sqrt, ['rstd2'], ['rstd2'])
                    S.op('dve', lambda E: E.reciprocal(rstd2[:], rstd2[:]), ['rstd2'], ['rstd2'])
                    o_t = ob[sub]
                    ko = 'ob%d' % sub
                    TS(S, 'dve', o_t[:], y2[:], mv2[:, 0:1], rstd2[:], ALU.subtract, ALU.mult,
                       [('y2', 0), ('y2', 1), 'mv2', 'rstd2'], [ko])
                    TT(S, 'pool', o_t[:], o_t[:], ln2g, ALU.mult, [ko, 'bc2'], [ko])
                    TT(S, 'pool', o_t[:], o_t[:], ln2b, ALU.add, [ko, 'bc2'], [ko])
                    S.dma('sp', out_d[r0:r0 + 128, :], o_t[:], reads=[ko], is_output=True)
                S = S_real2
                if EQ_prev is not None:
                    while not EQ_prev.empty():
                        EQ_prev.pop_emit()
                EQ_prev = EQ
            while not EQ_prev.empty():
                EQ_prev.pop_emit()

        S.finish()
        S.emit()
    return nc


def n_blk(g, s):
    return g * 4 + s


def _prep_shared(inp):
    f32 = np.float32
    w_in = np.asarray(inp["w_in"], f32)
    b_in = np.asarray(inp["b_in"], f32)
    qcols = []
    for c in range(4):
        qcols += list(range(c * 64, c * 64 + 64)) + list(range((4 + c) * 64, (4 + c) * 64 + 64))
    order = qcols + list(range(512, 640)) + list(range(768, 1792))
    w_fm = np.ascontiguousarray(w_in[:, order])
    b_fm = b_in[order].reshape(13, 128).T.copy()
    w_v = np.ascontiguousarray(w_in[:, 640:768])
    rows = np.concatenate([b_in[640:768], np.asarray(inp["b_out"], f32)])[None, :].copy()
    pos = np.arange(S_LEN, dtype=f32)
    inv_freq = (np.float32(500000.0) ** (-np.arange(0, 16, 2, dtype=f32) / np.float32(16))).astype(f32)
    ang = (pos[:, None] * inv_freq[None, :]).astype(f32)
    cosv = np.cos(ang).astype(f32).T
    sinv = np.sin(ang).astype(f32).T
    cos_t = np.ones((128, S_LEN), f32)
    sin_t = np.zeros((128, S_LEN), f32)
    perm = np.zeros((128, 128), f32)
    for blk in range(2):
        o = blk * 64
        cos_t[o:o + 8] = cosv
        cos_t[o + 8:o + 16] = cosv
        sin_t[o:o + 8] = -sinv
        sin_t[o + 8:o + 16] = sinv
        for d in range(8):
            perm[o + d + 8, o + d] = 1.0
            perm[o + d, o + d + 8] = 1.0
    kk = np.arange(128)[:, None]
    qq = np.arange(128)[None, :]
    maskC = (qq >= kk).astype(f32)
    maskP = (kk > qq).astype(f32)
    ident = np.eye(128, dtype=f32)
    iota = np.broadcast_to(np.arange(128, dtype=f32)[None, :], (128, 128))
    consts = np.concatenate([perm, maskC, maskP, ident, iota], axis=1).astype(f32)
    ga = np.asarray(inp["gate_a_w"], f32)
    gx = np.asarray(inp["gate_x_w"], f32)
    w_bd = np.zeros((128, 8, 128), f32)
    for j in range(4):
        for hh_ in range(2):
            w_bd[hh_ * 64:(hh_ + 1) * 64, j, hh_ * 64:(hh_ + 1) * 64] = ga[2 * j + hh_]
            w_bd[hh_ * 64:(hh_ + 1) * 64, 4 + j, hh_ * 64:(hh_ + 1) * 64] = gx[2 * j + hh_]
    rnnp = np.zeros((128, 4, 9), f32)
    cw = np.asarray(inp["conv_w"], f32)
    for k in range(4):
        rnnp[:, :, k] = cw[k].reshape(4, 128).T
    for idx, nm in [(4, "conv_b"), (5, "gate_a_b"), (6, "gate_x_b"), (7, "lru_lambda"), (8, "norm_rnn_g")]:
        rnnp[:, :, idx] = np.asarray(inp[nm], f32).reshape(4, 128).T
    bcast = np.concatenate([np.asarray(inp[n], f32) for n in
                            ["norm_attn_g", "ln1_g", "ln1_b", "ln2_g", "ln2_b", "attn_sinks"]])[None, :].copy()
    keysT = np.concatenate([np.asarray(inp["peer_keys_1"], f32).T, np.asarray(inp["peer_keys_2"], f32).T], axis=1).copy()
    pu = np.asarray(inp["peer_u"], f32)
    u_l = np.ascontiguousarray(pu.reshape(128, 128, 8, 128).transpose(0, 3, 2, 1)).reshape(16384, 1024)
    return {
        "w_fm": w_fm, "w_v": w_v, "b_fm": b_fm, "rows": rows, "cos_t": cos_t, "sin_t": sin_t, "consts": consts,
        "w_bd": w_bd.reshape(128, 1024), "rnnp": rnnp.reshape(128, 36), "bcast": bcast,
        "w_out": np.ascontiguousarray(np.asarray(inp["w_out"], f32)),
        "w_q": np.ascontiguousarray(np.asarray(inp["peer_w_q"], f32)), "keysT": keysT,
        "u_l": u_l, "v_l": np.ascontiguousarray(np.asarray(inp["peer_v"], f32)),
    }


_NC_CACHE = {}


def kernel(**inputs):
    x = np.asarray(inputs["x"], np.float32)
    shared = _prep_shared(inputs)
    if "nc" not in _NC_CACHE:
        _NC_CACHE["nc"] = build_program()
    nc = _NC_CACHE["nc"]
    in_maps = []
    for b in range(NCORES):
        m = dict(shared)
        m["x_tok"] = np.ascontiguousarray(x[b])
        m["xT"] = np.ascontiguousarray(x[b].T)
        in_maps.append(m)
    res = run_bass_kernel_spmd(nc, in_maps, core_ids=list(range(NCORES)))
    out = np.stack([np.asarray(r["out"], np.float32) for r in res.results], axis=0)
    if DEBUG:
        kernel.dbg = [{k: np.asarray(v) for k, v in r.items() if k.startswith("dbg")} for r in res.results]
    return out
```

```python
import os
import numpy as np
from contextlib import ExitStack
import concourse.bass as bass
import concourse.mybir as mybir
from concourse.bass_utils import run_bass_kernel_spmd

F32 = mybir.dt.float32
BF16 = mybir.dt.bfloat16
U32 = mybir.dt.uint32
AF = mybir.ActivationFunctionType
ALU = mybir.AluOpType
AX = mybir.AxisListType

S_LEN = 4096
D = 1024
NCORES = 8
ALPHA = 2.0 ** 0.25
EPS = 1e-5
NE1 = int(os.environ.get("K_NE1", "128"))
DEBUG = os.environ.get("K_DEBUG", "")

ENGS = ['pe', 'dve', 'act', 'pool', 'sp']
NDMA = 96
NDMA_HW = 64


class Sched:
    def __init__(self, nc, es):
        self.nc = nc
        self.sems = {e: es.enter_context(nc.semaphore("sem_" + e)) for e in ENGS}
        self.cnt = {e: 0 for e in ENGS}
        self.prog = {e: [] for e in ENGS}
        self.seen = {e: {f: 0 for f in ENGS} for e in ENGS}
        self.dseen = {e: {} for e in ENGS}
        self.lastw = {}
        self.rd = {}
        self.dma_sems = [es.enter_context(nc.semaphore("dsem%d" % i)) for i in range(NDMA)]
        self.dma_use = [0] * NDMA
        self.dma_next = 0
        self.dma_next_sw = 0
        self.out_tokens = []
        self.hook = None
        self._in_hook = False

    def _run_hook(self):
        if self.hook is not None and not self._in_hook:
            self._in_hook = True
            try:
                self.hook()
            finally:
                self._in_hook = False

    def _wait(self, eng, tok, kind, is_dma_issue=False):
        if tok[0] == 'e':
            _, e2, seq = tok
            if e2 == eng and not is_dma_issue:
                if eng == 'pe':
                    return
            if self.seen[eng][e2] >= seq:
                return
            self.seen[eng][e2] = seq
            sem = self.sems[e2]
            self.prog[eng].append(lambda E, sem=sem, seq=seq: E.wait_ge(sem, seq))
        else:
            _, idx, val = tok
            if self.dseen[eng].get(idx, 0) >= val:
                return
            self.dseen[eng][idx] = val
            sem = self.dma_sems[idx]
            self.prog[eng].append(lambda E, sem=sem, val=val: E.wait_ge(sem, val))

    def _deps(self, eng, reads, writes, is_dma_issue=False):
        for r in reads:
            t = self.lastw.get(r)
            if t is not None:
                self._wait(eng, t, 'raw', is_dma_issue)
        for w in writes:
            t = self.lastw.get(w)
            if t is not None:
                self._wait(eng, t, 'waw', is_dma_issue)
            for t in self.rd.get(w, ()):
                self._wait(eng, t, 'war', is_dma_issue)

    def _record(self, tok, reads, writes):
        for r in reads:
            lst = self.rd.setdefault(r, [])
            if tok[0] == 'e':
                lst[:] = [t for t in lst if not (t[0] == 'e' and t[1] == tok[1])]
            lst.append(tok)
        for w in writes:
            self.lastw[w] = tok
            self.rd[w] = []

    def op(self, eng, fn, reads=(), writes=(), cost=None):
        self._deps(eng, reads, writes)
        self.cnt[eng] += 1
        seq = self.cnt[eng]
        sem = self.sems[eng]
        self.prog[eng].append(lambda E, fn=fn, sem=sem: fn(E).then_inc(sem, 1))
        self._record(('e', eng, seq), reads, writes)
        self._run_hook()

    def dma(self, eng, out, in_, reads=(), writes=(), is_output=False, cost=None):
        self._deps(eng, reads, writes, is_dma_issue=True)
        if eng == 'pool':
            idx = NDMA_HW + self.dma_next_sw
            self.dma_next_sw = (self.dma_next_sw + 1) % (NDMA - NDMA_HW)
        else:
            idx = self.dma_next
            self.dma_next = (self.dma_next + 1) % NDMA_HW
        self.dma_use[idx] += 1
        n = self.dma_use[idx]
        if n > 1:
            self._wait(eng, ('d', idx, 16 * (n - 1)), 'waw', True)
        sem = self.dma_sems[idx]
        self.prog[eng].append(lambda E, sem=sem, out=out, in_=in_: E.dma_start(out=out, in_=in_).then_inc(sem, 16))
        tok = ('d', idx, 16 * n)
        self._record(tok, reads, writes)
        if is_output:
            self.out_tokens.append(tok)

    def barrier(self):
        for e in ENGS:
            for f in ENGS:
                if f != e and self.cnt[f] > 0:
                    self._wait(e, ('e', f, self.cnt[f]), 'raw', True)
            for idx in range(NDMA):
                if self.dma_use[idx] > 0:
                    self._wait(e, ('d', idx, 16 * self.dma_use[idx]), 'raw', True)

    def finish(self):
        for tok in self.out_tokens:
            self._wait('sp', tok, 'raw', True)
        for e in ENGS:
            if e != 'sp' and self.cnt[e] > 0:
                self._wait('sp', ('e', e, self.cnt[e]), 'raw', True)

    def emit(self):
        nc = self.nc
        prog = self.prog
        with nc.Block() as block:
            @block.tensor
            def _(E):
                for f in prog['pe']:
                    f(E)

            @block.vector
            def _(E):
                for f in prog['dve']:
                    f(E)

            @block.scalar
            def _(E):
                for f in prog['act']:
                    f(E)

            @block.gpsimd
            def _(E):
                for f in prog['pool']:
                    f(E)

            @block.sync
            def _(E):
                for f in prog['sp']:
                    f(E)


DEF_COST = {'pe': 0.4, 'dve': 0.3, 'act': 0.45, 'pool': 0.9, 'sp': 0.1}


def _fs(ap):
    try:
        return int(ap.free_size())
    except Exception:
        return 512


class Deferred:
    def __init__(self, S):
        self.S = S
        self.q = []
        self.pos = 0

    def op(self, eng, fn, reads=(), writes=(), cost=None):
        self.q.append(('op', eng, fn, tuple(reads), tuple(writes), False, DEF_COST[eng] if cost is None else cost))

    def dma(self, eng, out, in_, reads=(), writes=(), is_output=False, cost=None):
        self.q.append(('dma', eng, (out, in_), tuple(reads), tuple(writes), is_output, 2.0 if cost is None else cost))

    def empty(self):
        return self.pos >= len(self.q)

    def peek(self):
        return self.q[self.pos]

    def pop_emit(self):
        it = self.q[self.pos]
        self.pos += 1
        if it[0] == 'op':
            self.S.op(it[1], it[2], it[3], it[4])
        else:
            self.S.dma(it[1], it[2][0], it[2][1], reads=it[3], writes=it[4], is_output=it[5])
        return it


class Sim:
    LAT = 1.6

    def __init__(self):
        self.free = {e: 0.0 for e in ENGS}
        self.wfin = {}
        self.rfin = {}

    def est_start(self, it):
        t = self.free[it[1]]
        for r in it[3]:
            t = max(t, self.wfin.get(r, 0.0) + self.LAT)
        for w in it[4]:
            t = max(t, self.wfin.get(w, 0.0) + self.LAT, self.rfin.get(w, 0.0) + self.LAT)
        return t

    def commit(self, it, start):
        eng = it[1]
        if it[0] == 'dma':
            self.free[eng] = start + 0.06
            fin = start + it[6]
        else:
            fin = start + it[6]
            self.free[eng] = fin
        for r in it[3]:
            if self.rfin.get(r, 0.0) < fin:
                self.rfin[r] = fin
        for w in it[4]:
            self.wfin[w] = fin
            self.rfin[w] = 0.0


def merge_emit(sim, streams, must_drain):
    while any(streams[i] is not None and not streams[i].empty() for i in must_drain):
        best = None
        for i, q in enumerate(streams):
            if q is None or q.empty():
                continue
            t = sim.est_start(q.peek())
            if best is None or t < best[0] - 1e-9:
                best = (t, i)
        t, i = best
        it = streams[i].pop_emit()
        sim.commit(it, t)


def MM(S, out, pairs, reads, writes, start=True, stop=True):
    cost = sum(max(_fs(r), 64) / 2400.0 + 0.012 for (_, r) in pairs) + 0.15

    def fn(E):
        n = len(pairs)
        ins = None
        for i, (l, r) in enumerate(pairs):
            ins = E.matmul(out, l, r, start=(start and i == 0), stop=(stop and i == n - 1))
        return ins
    S.op('pe', fn, reads, writes, cost=cost)


def TR(S, out, in_, ident, reads, writes):
    S.op('pe', lambda E: E.transpose(out, in_, ident), reads, writes)


def TRN(S, outs_ins, ident, reads, writes):
    def fn(E):
        ins = None
        for o, i in outs_ins:
            ins = E.transpose(o, i, ident)
        return ins
    S.op('pe', fn, reads, writes)


def ACT(S, out, in_, func, reads, writes, bias=None, scale=None, accum_out=None):
    kw = {}
    if bias is not None:
        kw['bias'] = bias
    if scale is not None:
        kw['scale'] = scale
    if accum_out is not None:
        kw['accum_out'] = accum_out
    S.op('act', lambda E: E.activation(out, in_, func, **kw), reads, writes,
         cost=(230 + _fs(out)) / 1400.0 + (0.0 if func in (AF.Identity, AF.Square) else 0.7))


def _ecost(eng, out):
    n = _fs(out)
    if eng == 'pool':
        return (120 + 2.5 * n) / 1200.0
    return (70 + n) / 960.0


def TT(S, eng, out, a, b, op, reads, writes):
    S.op(eng, lambda E: E.tensor_tensor(out, a, b, op), reads, writes, cost=_ecost(eng, out))


def TS(S, eng, out, a, s1, s2, op0, op1, reads, writes):
    if op1 is None:
        S.op(eng, lambda E: E.tensor_scalar(out, a, s1, s2, op0), reads, writes, cost=_ecost(eng, out))
    else:
        S.op(eng, lambda E: E.tensor_scalar(out, a, s1, s2, op0, op1), reads, writes, cost=_ecost(eng, out))


def STT(S, eng, out, in0, scalar, in1, op0, op1, reads, writes):
    S.op(eng, lambda E: E.scalar_tensor_tensor(out, in0, scalar, in1, op0, op1), reads, writes, cost=_ecost(eng, out))


def CP(S, eng, out, in_, reads, writes):
    if eng == 'act':
        S.op('act', lambda E: E.activation(out, in_, AF.Identity), reads, writes, cost=(230 + _fs(out)) / 1400.0)
    else:
        S.op(eng, lambda E: E.tensor_copy(out, in_), reads, writes, cost=_ecost(eng, out))


class Rot:
    def __init__(self, items):
        self.items = items
        self.i = 0

    def next(self):
        it = self.items[self.i % len(self.items)]
        self.i += 1
        return it


def build_program():
    nc = bass.Bass("TRN2", target_bir_lowering=False)
    dt_in = lambda name, shape: nc.dram_tensor(name, shape, F32, kind="ExternalInput").ap()
    x_tok_d = dt_in("x_tok", [S_LEN, D])
    xT_d = dt_in("xT", [D, S_LEN])
    wfm_d = dt_in("w_fm", [D, 1664])
    wv_d = dt_in("w_v", [D, 128])
    bfm_d = dt_in("b_fm", [128, 13])
    rows_d = dt_in("rows", [1, 128 + 1024])
    cos_d = dt_in("cos_t", [128, S_LEN])
    sin_d = dt_in("sin_t", [128, S_LEN])
    consts_d = dt_in("consts", [128, 128 * 5])
    wbd_d = dt_in("w_bd", [128, 8 * 128])
    rnnp_d = dt_in("rnnp", [128, 4 * 9])
    bc_d = dt_in("bcast", [1, 512 + 4 * 1024 + 8])
    wout_d = dt_in("w_out", [D, D])
    wq_d = dt_in("w_q", [D, 2048])
    keys_d = dt_in("keysT", [128, 256])
    u_d = dt_in("u_l", [16384, 1024])
    v_d = dt_in("v_l", [16384, 1024])
    out_d = nc.dram_tensor("out", [S_LEN, D], F32, kind="ExternalOutput").ap()
    dbg = {}
    if DEBUG:
        dbg['h1'] = nc.dram_tensor("dbg_h1", [S_LEN, D], F32, kind="ExternalOutput").ap()
        dbg['route'] = nc.dram_tensor("dbg_route", [128, 3 * S_LEN], BF16, kind="ExternalOutput").ap()

    h1_d = nc.dram_tensor("h1_scr", [S_LEN, D], F32).ap()
    h1T_d = nc.dram_tensor("h1T_scr", [D, S_LEN], BF16).ap()
    ubf_d = nc.dram_tensor("u_bf", [16384, 1024], BF16).ap()
    vbf_d = nc.dram_tensor("v_bf", [16384, 1024], BF16).ap()

    with ExitStack() as es:
        S = Sched(nc, es)

        def T(stack, name, shape, dt=F32):
            return stack.enter_context(nc.sbuf_tensor("sb_" + name, shape, dt))

        consts_f = T(es, "consts_f", [128, 640])
        consts_b = T(es, "consts_b", [128, 640], BF16)
        S.dma('sp', consts_f[:], consts_d, writes=['consts_f'])
        S.dma('pool', consts_b[:], consts_d, writes=['consts_b'])
        perm_b = consts_b[:, 0:128]
        maskC_b = consts_b[:, 128:256]
        maskP_b = consts_b[:, 256:384]
        ident_b = consts_b[:, 384:512]
        iota_b = consts_b[:, 512:640]
        ident_f = consts_f[:, 384:512]
        iota16_f = consts_f[:, 512:528]
        ones_f = T(es, "ones_f", [128, 128])
        S.op('dve', lambda E: E.memset(ones_f[:], 1.0), writes=['ones_f'])
        ones_b = T(es, "ones_b", [128, 128], BF16)
        S.op('dve', lambda E: E.memset(ones_b[:], 1.0), writes=['ones_b'])
        esink = T(es, "esink", [128, 8])
        S.dma('sp', esink[:], bc_d[:, 4608:4616].partition_broadcast(128), writes=['esink'])
        ACT(S, esink[:], esink[:], AF.Exp, ['esink'], ['esink'])
        route_d = nc.dram_tensor("route_scr", [128, 3, S_LEN], BF16).ap()
        wqb_d = nc.dram_tensor("wq_bf", [D, 2048], BF16).ap()

        NCH = 32
        rows_per = 16384 // NCH
        cast_jobs = []
        for i in range(NCH):
            sl = slice(i * rows_per, (i + 1) * rows_per)
            cast_jobs.append((ubf_d[sl, :], u_d[sl, :], ('ubf', i)))
            cast_jobs.append((vbf_d[sl, :], v_d[sl, :], ('vbf', i)))

        def issue_casts(k):
            for _ in range(k):
                if cast_jobs:
                    o_, i_, key = cast_jobs.pop(0)
                    S.dma('pool', o_, i_, writes=[key])

        with ExitStack() as p1:
            banks = [p1.enter_context(nc.psum_tensor("p1bank%d" % i, [128, 512], F32)) for i in range(7)]
            bankb = p1.enter_context(nc.psum_tensor("p1bankb", [128, 1024], BF16))
            wfm = T(p1, "wfm", [128, 8, 1664], BF16)
            wv = T(p1, "wv", [128, 8, 128], BF16)
            wout = T(p1, "wout", [128, 8, 1024], BF16)
            for kc in range(8):
                S.dma('pool', wfm[:, kc, :], wfm_d[kc * 128:(kc + 1) * 128, :], writes=['wfm'])
            S.dma('pool', wv[:], wv_d.rearrange("(c p) n -> p c n", p=128), writes=['wv'])
            for kc in range(8):
                S.dma('pool', wout[:, kc, :], wout_d[kc * 128:(kc + 1) * 128, :], writes=['wout'])
            bc = T(p1, "bc", [128, 2560])
            S.dma('sp', bc[:], bc_d[:, 0:2560].partition_broadcast(128), writes=['bc'])
            ga_bc = bc[:, 0:512]
            ln1g = bc[:, 512:1536]
            ln1b = bc[:, 1536:2560]
            keysT = T(p1, "keysT", [128, 256], BF16)
            S.dma('pool', keysT[:], keys_d, writes=['keysT'])
            wqs = [T(p1, "wqs%d" % i, [128, 8, 256], BF16) for i in range(2)]
            qpT = T(p1, "qpT", [128, 16, 512], BF16)
            stm = T(p1, "stm", [128, 16, 128])
            vtop = T(p1, "vtop", [128, 16, 16])
            itop = T(p1, "itop", [128, 16, 16], U32)
            itopf = T(p1, "itopf", [128, 16, 16])
            cand = T(p1, "cand", [128, 8, 256])
            sc = T(p1, "sc", [128, 8, 16])
            ci = T(p1, "ci", [128, 8, 16], U32)
            aki = T(p1, "aki", [128, 8, 16], U32)
            bki = T(p1, "bki", [128, 8, 16], U32)
            akf = T(p1, "akf", [128, 8, 16])
            bkf = T(p1, "bkf", [128, 8, 16])
            e1f = T(p1, "e1f", [128, 128])
            e2f = T(p1, "e2f", [128, 128])
            gex = T(p1, "gex", [128, 8, 16])
            gz = T(p1, "gz", [128, 8])
            gf = T(p1, "gf", [128, 128])
            rst = T(p1, "rst", [128, 3, 128], BF16)
            bfm = T(p1, "bfm", [128, 13])
            S.dma('sp', bfm[:], bfm_d, writes=['bfm'])
            for kc in range(8):
                S.dma('pool', wqb_d[kc * 128:(kc + 1) * 128, :], wq_d[kc * 128:(kc + 1) * 128, :], writes=['wqb'])
            rows_b = T(p1, "rows_b", [1, 1152], BF16)
            S.dma('pool', rows_b[:], rows_d, writes=['rows_b'])
            wbd = T(p1, "wbd", [128, 8, 128], BF16)
            S.dma('pool', wbd[:], wbd_d.rearrange("p (j n) -> p j n", j=8), writes=['wbd'])
            rnnp = T(p1, "rnnp", [128, 4, 9])
            S.dma('sp', rnnp[:], rnnp_d.rearrange("p (j n) -> p j n", j=4), writes=['rnnp'])
            cL = T(p1, "cL", [128, 4])
            ACT(S, cL[:], rnnp[:, :, 7], AF.Exp, ['rnnp'], ['cL'], scale=-1.0)
            ACT(S, cL[:], cL[:], AF.Ln, ['cL'], ['cL'], bias=1.0)
            TS(S, 'dve', cL[:], cL[:], -8.0, None, ALU.mult, None, ['cL'], ['cL'])
            cLh = T(p1, "cLh", [128, 4])
            TS(S, 'dve', cLh[:], cL[:], 0.5, None, ALU.mult, None, ['cL'], ['cLh'])
            hb = T(p1, "hb", [128, 4, 2])
            TS(S, 'dve', hb[:], rnnp[:, :, 5:7], 0.5, None, ALU.mult, None, ['rnnp'], ['hb'])

            xTb = [T(p1, "xTb%d" % i, [128, 8, 512], BF16) for i in range(1)]
            xtok = [T(p1, "xtok%d" % i, [128, 1024]) for i in range(1)]
            cs = [T(p1, "cs%d" % i, [128, 2, 512]) for i in range(1)]
            zq = T(p1, "zq", [128, 512])
            zqb = T(p1, "zqb", [128, 512], BF16)
            t1 = T(p1, "t1", [128, 512])
            t2 = T(p1, "t2", [128, 512])
            qrot = T(p1, "qrot", [128, 5, 512], BF16)
            kprev = T(p1, "kprev", [128, 128], BF16)
            vaug = T(p1, "vaug", [128, 5, 2, 65], BF16)
            S.op('dve', lambda E: E.memset(vaug[:], 1.0), writes=['vaug%d' % i for i in range(5)])
            ex = [T(p1, "ex%d" % i, [128, 2, 512], BF16) for i in range(2)]
            den = T(p1, "den", [128, 8])
            attn = T(p1, "attn", [128, 512])
            st1 = T(p1, "st1", [128, 4])
            attn_n = T(p1, "attn_n", [128, 512], BF16)
            mixT = T(p1, "mixT", [128, 8, 512], BF16)
            xr = T(p1, "xr", [128, 4, 515])
            S.op('dve', lambda E: E.memset(xr[:], 0.0), writes=['xr%d' % j for j in range(4)])
            gl = T(p1, "gl", [128, 512])
            xc = T(p1, "xc", [128, 512])
            rr = T(p1, "rr", [128, 512])
            ii = T(p1, "ii", [128, 512])
            bb = T(p1, "bb", [128, 512])
            hh = T(p1, "hh", [128, 512])
            hstate = T(p1, "hstate", [128, 4])
            ro = T(p1, "ro", [128, 4, 512], BF16)
            y = T(p1, "y", [128, 1024])
            bst = T(p1, "bst", [128, 2, 6])
            mv = T(p1, "mv", [128, 2])
            rstd = T(p1, "rstd", [128, 1])
            h1 = [T(p1, "h1_%d" % i, [128, 1024]) for i in range(1)]
            h1b = T(p1, "h1b", [128, 1024], BF16)
            h1Tg = [T(p1, "h1Tg%d" % i, [128, 8, 512], BF16) for i in range(2)]

            pool = Rot([(banks[i], 'bank%d' % i) for i in range(4)])
            rpool = Rot([(banks[i], 'bank%d' % i) for i in (4, 5)])
            ssq_bank, ssq_key = banks[6], 'bank6'


            def emit_rpre(R, g, h1T_g, kht):
                for piece in range(8):
                    wq_t = wqs[piece % 2]
                    kwq = 'wqs%d' % (piece % 2)
                    R.dma('sp', wq_t[:], wqb_d.rearrange("(c p) n -> p c n", p=128)[:, :, piece * 256:(piece + 1) * 256],
                          reads=['wqb'], writes=[kwq])
                    for i in range(2):
                        cc = piece * 2 + i
                        pq, kq = rpool.next()
                        MM(R, pq[:], [(wq_t[:, kc, i * 128:(i + 1) * 128], h1T_g[:, kc, :]) for kc in range(8)],
                           [kwq, kht], [kq])
                        CP(R, 'act', qpT[:, cc, :], pq[:], [kq], [('r_qpT', cc)])

            def emit_routing(R, g, s_, h1T_g, kht):
                qs = slice(s_ * 128, (s_ + 1) * 128)
                blk = g * 4 + s_
                tok = slice(blk * 128, (blk + 1) * 128)
                for q4 in range(4):
                    pb, kb_ = rpool.next()
                    pbv = pb[:].rearrange("p (c e) -> p c e", c=4)

                    def scfn(E, pbv=pbv, q4=q4, qs=qs):
                        ins = None
                        for i in range(4):
                            cc = q4 * 4 + i
                            half = cc % 2
                            ins = E.matmul(pbv[:, i, :], qpT[:, cc, qs], keysT[:, half * 128:(half + 1) * 128],
                                           start=True, stop=True)
                        return ins
                    R.op('pe', scfn, [('r_qpT', q4 * 4 + i) for i in range(4)] + ['keysT'], [kb_])
                    CP(R, 'act', stm[:, q4 * 4:(q4 + 1) * 4, :], pbv, [kb_], [('r_stm', q4 * 4 + i) for i in range(4)])
                for blk4 in range(4):
                    ccs = range(blk4 * 4, blk4 * 4 + 4)
                    for cc in ccs:
                        R.op('dve', lambda E, cc=cc: E.max(vtop[:, cc, 0:8], stm[:, cc, :]), [('r_stm', cc)], [('r_vtop', cc)])
                    for cc in ccs:
                        R.op('dve', lambda E, cc=cc: E.max_index(itop[:, cc, 0:8], vtop[:, cc, 0:8], stm[:, cc, :]),
                             [('r_stm', cc), ('r_vtop', cc)], [('r_itop', cc)])
                    for cc in ccs:
                        R.op('dve', lambda E, cc=cc: E.match_replace(stm[:, cc, :], vtop[:, cc, 0:8], stm[:, cc, :], -1e30),
                             [('r_stm', cc), ('r_vtop', cc)], [('r_stm', cc)])
                    for cc in ccs:
                        R.op('dve', lambda E, cc=cc: E.max(vtop[:, cc, 8:16], stm[:, cc, :]), [('r_stm', cc)], [('r_vtop', cc)])
                    for cc in ccs:
                        R.op('dve', lambda E, cc=cc: E.max_index(itop[:, cc, 8:16], vtop[:, cc, 8:16], stm[:, cc, :]),
                             [('r_stm', cc), ('r_vtop', cc)], [('r_itop', cc)])
                allv = [('r_vtop', cc) for cc in range(16)]
                alli = [('r_itop', cc) for cc in range(16)]
                allstm = [('r_stm', cc) for cc in range(16)]
                CP(R, 'dve', itopf[:], itop[:], alli, ['r_itopf'])
                v4 = vtop[:].rearrange("p (h f) a -> p h f a", f=2)
                i4 = itopf[:].rearrange("p (h f) a -> p h f a", f=2)
                TT(R, 'dve', cand[:].rearrange("p h (a b) -> p h a b", a=16),
                   v4[:, :, 0, :].unsqueeze(3).to_broadcast([128, 8, 16, 16]),
                   v4[:, :, 1, :].unsqueeze(2).to_broadcast([128, 8, 16, 16]), ALU.add, allv, ['r_cand'])
                for blk2 in range(2):
                    hs = range(blk2 * 4, blk2 * 4 + 4)
                    for h in hs:
                        R.op('dve', lambda E, h=h: E.max(sc[:, h, 0:8], cand[:, h, :]), ['r_cand'], [('r_sc', h)])
                    for h in hs:
                        R.op('dve', lambda E, h=h: E.max_index(ci[:, h, 0:8], sc[:, h, 0:8], cand[:, h, :]),
                             ['r_cand', ('r_sc', h)], [('r_ci', h)])
                    for h in hs:
                        R.op('dve', lambda E, h=h: E.match_replace(cand[:, h, :], sc[:, h, 0:8], cand[:, h, :], -1e30),
                             ['r_cand', ('r_sc', h), ('r_ci', h)], [('r_cw', h)])
                    for h in hs:
                        R.op('dve', lambda E, h=h: E.max(sc[:, h, 8:16], cand[:, h, :]), [('r_cw', h)], [('r_sc', h)])
                    for h in hs:
                        R.op('dve', lambda E, h=h: E.max_index(ci[:, h, 8:16], sc[:, h, 8:16], cand[:, h, :]),
                             [('r_cw', h), ('r_sc', h)], [('r_ci', h)])
                allsc = [('r_sc', h) for h in range(8)]
                allci = [('r_ci', h) for h in range(8)]
                allcw = [('r_cw', h) for h in range(8)]
                R.op('dve', lambda E: E.tensor_single_scalar(aki[:], ci[:], 4, ALU.logical_shift_right), allci, ['r_aki'])
                R.op('dve', lambda E: E.tensor_single_scalar(bki[:], ci[:], 15, ALU.bitwise_and), allci, ['r_bki'])
                CP(R, 'dve', akf[:], aki[:], ['r_aki'], ['r_akf'])
                CP(R, 'dve', bkf[:], bki[:], ['r_bki'], ['r_bkf'])
                TT(R, 'dve', gex[:], sc[:], sc[:, :, 0:1].to_broadcast([128, 8, 16]), ALU.subtract, allsc, ['r_gex'])
                ACT(R, gex[:], gex[:], AF.Exp, ['r_gex'], ['r_gex'])
                R.op('dve', lambda E: E.tensor_reduce(gz[:], gex[:], AX.X, ALU.add), ['r_gex'], ['r_gz'])
                R.op('dve', lambda E: E.reciprocal(gz[:], gz[:]), ['r_gz'], ['r_gz'])
                TT(R, 'dve', gf[:].rearrange("p (h k) -> p h k", h=8), gex[:],
                   gz[:].unsqueeze(2).to_broadcast([128, 8, 16]), ALU.mult, ['r_gex', 'r_gz'], ['r_gf'])
                eqv = stm[:].rearrange("p c e -> p (c e)").rearrange("p (h k a) -> p h k a", h=8, k=16)
                eq2v = cand[:].rearrange("p h (k a) -> p h k a", k=16)
                iob = iota16_f.unsqueeze(1).unsqueeze(1).to_broadcast([128, 8, 16, 16])
                TT(R, 'dve', eqv, akf[:].unsqueeze(3).to_broadcast([128, 8, 16, 16]), iob, ALU.is_equal,
                   ['r_akf', 'consts_f'] + allstm + alli, allstm)
                TT(R, 'dve', eqv, eqv, i4[:, :, 0, :].unsqueeze(2).to_broadcast([128, 8, 16, 16]), ALU.mult,
                   allstm + ['r_itopf'], allstm)
                R.op('dve', lambda E: E.tensor_reduce(e1f[:].rearrange("p (h k) -> p h k", h=8), eqv, AX.X, ALU.add),
                     allstm, ['r_e1f'])
                TT(R, 'dve', eq2v, bkf[:].unsqueeze(3).to_broadcast([128, 8, 16, 16]), iob, ALU.is_equal,
                   ['r_bkf', 'consts_f', 'r_cand'] + allcw + allci, ['r_cand'] + allcw)
                TT(R, 'dve', eq2v, eq2v, i4[:, :, 1, :].unsqueeze(2).to_broadcast([128, 8, 16, 16]), ALU.mult,
                   ['r_cand', 'r_itopf'] + allcw, ['r_cand'] + allcw)
                R.op('dve', lambda E: E.tensor_reduce(e2f[:].rearrange("p (h k) -> p h k", h=8), eq2v, AX.X, ALU.add),
                     ['r_cand'] + allcw, ['r_e2f'])
                for i3, (src, ksrc) in enumerate([(e1f, 'r_e1f'), (e2f, 'r_e2f'), (gf, 'r_gf')]):
                    ptr, ktr = rpool.next()
                    TR(R, ptr[:, 0:128], src[:], ident_f, [ksrc, 'consts_f'], [ktr])
                    CP(R, 'act', rst[:, i3, :], ptr[:, 0:128], [ktr], [('r_rst', i3)])
                R.dma('sp', route_d[:, :, tok], rst[:], reads=[('r_rst', i) for i in range(3)], writes=[('route', blk)])

            NG = S_LEN // 512

            def load_x(g_):
                t0_ = g_ * 512
                S.dma('pool', xTb[0][:], xT_d.rearrange("(c p) t -> p c t", p=128)[:, :, t0_:t0_ + 512],
                      writes=['xTb0'])

            def load_cs(g_):
                t0_ = g_ * 512
                S.dma('sp', cs[0][:, 0, :], cos_d[:, t0_:t0_ + 512], writes=['cs0'])
                S.dma('sp', cs[0][:, 1, :], sin_d[:, t0_:t0_ + 512], writes=['cs0'])

            S_real = S
            sim = Sim()
            QR_prev = None
            poolA = Rot([(banks[i], 'bank%d' % i) for i in (0, 1)])
            poolB = Rot([(banks[i], 'bank%d' % i) for i in (2, 3)])
            poolC = Rot([(banks[i], 'bank%d' % i) for i in range(4)])
            for g in range(NG):
                t0 = g * 512
                xT_g = xTb[0]
                kx = 'xTb0'
                cs_g = cs[0]
                kcs = 'cs0'
                if g == 0:
                    load_x(0)
                    load_cs(0)
                issue_casts(8)
                QA, QB, QC, QR = Deferred(S_real), Deferred(S_real), Deferred(S_real), Deferred(S_real)
                S = QA
                pool = poolA

                for c in range(5):
                    pz, kz = pool.next()
                    MM(S, pz[:], [(wfm[:, kc, c * 128:(c + 1) * 128], xT_g[:, kc, :]) for kc in range(8)],
                       ['wfm', kx], [kz])
                    ACT(S, zq[:], pz[:], AF.Identity, [kz, 'bfm'], ['zq'], bias=bfm[:, c:c + 1])
                    CP(S, 'dve', attn_n[:], zq[:], ['zq'], ['attn_n'])
                    pp, kp = pool.next()
                    MM(S, pp[:], [(perm_b, attn_n[:])], ['consts_b', 'attn_n'], [kp])
                    TT(S, 'dve', t1[:], zq[:], cs_g[:, 0, :], ALU.mult, ['zq', kcs], ['t1'])
                    TT(S, 'dve', t2[:], pp[:], cs_g[:, 1, :], ALU.mult, [kp, kcs], ['t2'])
                    TT(S, 'dve', qrot[:, c, :], t1[:], t2[:], ALU.add, ['t1', 't2'], [('qrot', c)])

                if g + 1 < NG:
                    load_cs(g + 1)
                for s in range(4):
                    pv, kv = pool.next()
                    prs = [(xT_g[:, kc, s * 128:(s + 1) * 128], wv[:, kc, :]) for kc in range(8)]
                    prs.append((ones_b[0:1, 0:128], rows_b[0:1, 0:128]))
                    MM(S, pv[:, 0:128], prs, [kx, 'wv', 'ones_b', 'rows_b'], [kv])
                    CP(S, 'act', vaug[:, 1 + s, :, 0:64], pv[:, 0:128].rearrange("p (h d) -> p h d", h=2),
                       [kv], ['vaug%d' % (1 + s)])

                def attn_block(s):
                    n = 4 * g + s
                    qs = slice(s * 128, (s + 1) * 128)
                    for hk in range(2):
                        ps_ = slice(hk * 64, hk * 64 + 64)
                        e_t = ex[hk]
                        ke = 'ex%d' % hk
                        kbs = [1] if n == 0 else [0, 1]
                        for kb in kbs:
                            if kb == 1:
                                kT = qrot[ps_, 4, qs]
                                kread = ('qrot', 4)
                            elif s == 0:
                                kT = kprev[ps_, :]
                                kread = 'kprev'
                            else:
                                kT = qrot[ps_, 4, (s - 1) * 128:s * 128]
                                kread = ('qrot', 4)
                            psc, ksc = pool.next()
                            MM(S, psc[:].rearrange("p (g q) -> p g q", g=4), [(kT, qrot[ps_, 0:4, qs])],
                               [kread] + [('qrot', c) for c in range(4)], [ksc])
                            ACT(S, e_t[:, kb, :], psc[:], AF.Exp, [ksc], [(ke, kb)], scale=0.125)
                            mk = maskC_b if kb == 1 else maskP_b
                            TT(S, 'dve', e_t[:, kb, :].rearrange("p (g q) -> p g q", g=4),
                               e_t[:, kb, :].rearrange("p (g q) -> p g q", g=4),
                               mk.unsqueeze(1).to_broadcast([128, 4, 128]), ALU.mult,
                               [(ke, kb), 'consts_b'], [(ke, kb)])
                        po_, ko = pool.next()
                        pov = po_[:, 0:260].rearrange("p (g d) -> p g d", g=4)

                        def pvfn(E, pov=pov, e_t=e_t, kbs=kbs, s=s, hk=hk):
                            ins = None
                            for gi in range(4):
                                for j, kb in enumerate(kbs):
                                    slot = s + kb
                                    ins = E.matmul(pov[:, gi, :], e_t[:, kb, gi * 128:(gi + 1) * 128],
                                                   vaug[:, slot, hk, :], start=(j == 0), stop=(j == len(kbs) - 1))
                            return ins
                        S.op('pe', pvfn, [(ke, kb) for kb in kbs] + ['vaug%d' % (s + kb) for kb in kbs], [ko])
                        TT(S, 'dve', den[:, hk * 4:(hk + 1) * 4], pov[:, :, 64], esink[:, hk * 4:(hk + 1) * 4], ALU.add,
                           [ko, 'esink'], [('den', hk)])
                        S.op('dve', lambda E, hk=hk: E.reciprocal(den[:, hk * 4:(hk + 1) * 4], den[:, hk * 4:(hk + 1) * 4]),
                             [('den', hk)], [('den', hk)])
                        TT(S, 'dve', attn[:, hk * 256:(hk + 1) * 256].rearrange("p (g d) -> p g d", g=4),
                           pov[:, :, 0:64], den[:, hk * 4:(hk + 1) * 4].unsqueeze(2).to_broadcast([128, 4, 64]),
                           ALU.mult, [ko, ('den', hk)], [('attn', hk)])
                    ACT(S, y[:, 0:512], attn[:], AF.Square, [('attn', 0), ('attn', 1)], [('y', 0), 'st1a'],
                        accum_out=st1[:, 0:1])
                    TS(S, 'dve', st1[:, 1:2], st1[:, 0:1], 1.0 / 512, EPS, ALU.mult, ALU.add, ['st1a'], ['st1b'])
                    ACT(S, st1[:, 2:3], st1[:, 1:2], AF.Sqrt, ['st1b'], ['st1c'])
                    S.op('dve', lambda E: E.reciprocal(st1[:, 3:4], st1[:, 2:3]), ['st1c'], ['st1d'])
                    STT(S, 'dve', attn_n[:], attn[:], st1[:, 3:4], ga_bc, ALU.mult, ALU.mult,
                        [('attn', 0), ('attn', 1), 'st1d', 'bc'], ['attn_n'])
                    ptv = bankb[:, 0:512].rearrange("p (j q) -> p j q", j=4)
                    TRN(S, [(ptv[:, j, :], attn_n[:, j * 128:(j + 1) * 128]) for j in range(4)], ident_b,
                        ['attn_n', 'consts_b'], ['bankb'])
                    CP(S, 'act', mixT[:, 0:4, qs], ptv, ['bankb'], [('mixT', s)])

                def rnn_chunk(j):
                    pz, kz = pool.next()
                    MM(S, pz[:], [(wfm[:, kc, (5 + j) * 128:(6 + j) * 128], xT_g[:, kc, :]) for kc in range(8)],
                       ['wfm', kx], [kz])
                    kxr = 'xr%d' % j
                    ACT(S, xr[:, j, 3:515], pz[:], AF.Identity, [kz, 'bfm'], [kxr], bias=bfm[:, 5 + j:6 + j])
                    pz2, kz2 = pool.next()
                    MM(S, pz2[:], [(wfm[:, kc, (9 + j) * 128:(10 + j) * 128], xT_g[:, kc, :]) for kc in range(8)],
                       ['wfm', kx], [kz2])
                    ACT(S, gl[:], pz2[:], AF.Gelu_apprx_tanh, [kz2, 'bfm'], ['gl'], bias=bfm[:, 9 + j:10 + j])
                    TS(S, 'dve', xc[:], xr[:, j, 0:512], rnnp[:, j, 0:1], rnnp[:, j, 4:5], ALU.mult, ALU.add,
                       [kxr, 'rnnp'], ['xc'])
                    for k in range(1, 4):
                        STT(S, 'dve', xc[:], xr[:, j, k:k + 512], rnnp[:, j, k:k + 1], xc[:], ALU.mult, ALU.add,
                            [kxr, 'rnnp', 'xc'], ['xc'])
                    CP(S, 'dve', xr[:, j, 0:3], xr[:, j, 512:515], [kxr], [kxr])
                    CP(S, 'dve', zqb[:], xc[:], ['xc'], ['zqb'])
                    pa, ka = pool.next()
                    MM(S, pa[:], [(wbd[:, j, :], zqb[:])], ['wbd', 'zqb'], [ka])
                    pxx, kxx = pool.next()
                    MM(S, pxx[:], [(wbd[:, 4 + j, :], zqb[:])], ['wbd', 'zqb'], [kxx])
                    ACT(S, rr[:], pa[:], AF.Tanh, [ka, 'hb'], ['rr'], bias=hb[:, j, 0:1], scale=0.5)
                    ACT(S, ii[:], pxx[:], AF.Tanh, [kxx, 'hb'], ['ii'], bias=hb[:, j, 1:2], scale=0.5)
                    ACT(S, rr[:], rr[:], AF.Exp, ['rr', 'cLh'], ['rr'], scale=cLh[:, j:j + 1], bias=cLh[:, j:j + 1])
                    TT(S, 'dve', bb[:], rr[:], rr[:], ALU.mult, ['rr'], ['bb'])
                    ACT(S, bb[:], bb[:], AF.Sqrt, ['bb'], ['bb'], bias=0.25, scale=-0.25)
                    STT(S, 'dve', ii[:], ii[:], 1.0, xc[:], ALU.add, ALU.mult, ['ii', 'xc'], ['ii'])
                    TT(S, 'dve', bb[:], bb[:], ii[:], ALU.mult, ['bb', 'ii'], ['bb'])
                    init = 0.0 if g == 0 else hstate[:, j:j + 1]
                    S.op('dve', lambda E, init=init: E.tensor_tensor_scan(hh[:], rr[:], bb[:], init, ALU.mult, ALU.add),
                         ['rr', 'bb', ('hstate', j)], ['hh'])
                    CP(S, 'dve', hstate[:, j:j + 1], hh[:, 511:512], ['hh'], [('hstate', j)])
                    TT(S, 'dve', ro[:, j, :], gl[:], hh[:], ALU.mult, ['gl', 'hh'], [('ro', j)])
                    ACT(S, hh[:], ro[:, j, :], AF.Square, [('ro', j)], ['hh'])
                    MM(S, ssq_bank[:], [(ones_f[:], hh[:])], ['ones_f', 'hh'], [ssq_key], start=(j == 0), stop=(j == 3))
                for sj in range(4):
                    attn_block(sj)
                CP(S, 'pool', kprev[:], qrot[:, 4, 384:512], [('qrot', 4)], ['kprev'])
                CP(S, 'pool', vaug[:, 0, :, :], vaug[:, 4, :, :], ['vaug4'], ['vaug0'])

                S = QB
                pool = poolB
                for sj in range(4):
                    rnn_chunk(sj)
                TS(S, 'dve', gl[:], ssq_bank[:], 1.0 / 512, EPS, ALU.mult, ALU.add, [ssq_key], ['gl'])
                ACT(S, gl[:], gl[:], AF.Sqrt, ['gl'], ['gl'])
                S.op('dve', lambda E: E.reciprocal(gl[:], gl[:]), ['gl'], ['gl'])
                for j in range(4):
                    STT(S, 'dve', mixT[:, 4 + j, :], ro[:, j, :], rnnp[:, j, 8:9], gl[:], ALU.mult, ALU.mult,
                        [('ro', j), 'rnnp', 'gl'], [('mixTr', j)])

                S = QC
                pool = poolC
                h1T_g = h1Tg[g % 2]
                kht = 'h1Tg%d' % (g % 2)
                for s in range(4):
                    qs = slice(s * 128, (s + 1) * 128)
                    xt = xtok[0]
                    kxt = 'xtok0'
                    S.dma('sp', xt[:], x_tok_d[t0 + s * 128:t0 + (s + 1) * 128, :], writes=[kxt])
                    for half in range(2):
                        py, ky = pool.next()
                        cs_ = slice(half * 512, (half + 1) * 512)
                        prs = [(mixT[:, kc, qs], wout[:, kc, cs_]) for kc in range(8)]
                        prs.append((ones_b[0:1, 0:128], rows_b[0:1, 128 + half * 512:128 + (half + 1) * 512]))
                        MM(S, py[:], prs, [('mixT', s)] + [('mixTr', j) for j in range(4)] + ['wout', 'ones_b', 'rows_b'], [ky])
                        STT(S, 'dve', y[:, cs_], xt[:, cs_], ALPHA, py[:], ALU.mult, ALU.add, [kxt, ky], [('y', half)])
                        S.op('dve', lambda E, half=half, cs_=cs_: E.bn_stats(bst[:, half, :], y[:, cs_]),
                             [('y', half)], [('bst', half)])
                    S.op('dve', lambda E: E.bn_aggr(mv[:], bst[:].rearrange("p a b -> p (a b)")),
                         [('bst', 0), ('bst', 1)], ['mv'])
                    TS(S, 'dve', rstd[:], mv[:, 1:2], EPS, None, ALU.add, None, ['mv'], ['rstd'])
                    ACT(S, rstd[:], rstd[:], AF.Sqrt, ['rstd'], ['rstd'])
                    S.op('dve', lambda E: E.reciprocal(rstd[:], rstd[:]), ['rstd'], ['rstd'])
                    h1t = h1[0]
                    kh = 'h1_0'
                    STT(S, 'dve', h1t[:], y[:], mv[:, 0:1], ln1g, ALU.subtract, ALU.mult,
                        [('y', 0), ('y', 1), 'mv', 'bc'], [kh])
                    STT(S, 'dve', h1t[:], h1t[:], rstd[:], ln1b, ALU.mult, ALU.add, [kh, 'rstd', 'bc'], [kh])
                    S.dma('sp', h1_d[t0 + s * 128:t0 + (s + 1) * 128, :], h1t[:], reads=[kh], writes=[('h1d', n_blk(g, s))])
                    if DEBUG:
                        S.dma('sp', dbg['h1'][t0 + s * 128:t0 + (s + 1) * 128, :], h1t[:], reads=[kh], is_output=True)
                    CP(S, 'act', h1b[:], h1t[:], [kh], ['h1b'])
                    pt8 = bankb[:].rearrange("p (j q) -> p j q", j=8)
                    TRN(S, [(pt8[:, j, :], h1b[:, j * 128:(j + 1) * 128]) for j in range(8)], ident_b,
                        ['h1b', 'consts_b'], ['bankb'])
                    CP(S, 'act', h1T_g[:, :, qs], pt8, ['bankb'], [kht])
                S.dma('sp', h1T_d.rearrange("(c p) t -> p c t", p=128)[:, :, t0:t0 + 512], h1T_g[:], reads=[kht],
                      writes=[('h1Td', g)])
                emit_rpre(QR, g, h1T_g, kht)
                for s_r in range(4):
                    emit_routing(QR, g, s_r, h1T_g, kht)
                S = S_real
                merge_emit(sim, [QR_prev, QA, QB], [1, 2])
                if g + 1 < NG:
                    load_x(g + 1)
                merge_emit(sim, [QR_prev, QC], [0, 1])
                QR_prev = QR
            merge_emit(sim, [QR_prev], [0])
            pool = None
            if DEBUG:
                S.dma('sp', dbg['route'].rearrange("p (a t) -> p a t", a=3), route_d,
                      reads=[('route', b_) for b_ in range(32)], is_output=True)

        issue_casts(1000)
        S.barrier()
        with ExitStack() as p3:
            banks = [p3.enter_context(nc.psum_tensor("p3bank%d" % i, [128, 512], F32)) for i in range(8)]
            TT_ = 256
            NT = S_LEN // TT_
            assert NE1 == 128
            HALF = 64
            Gh = [T(p3, "Gh%d" % i, [128, TT_, HALF], BF16) for i in range(2)]
            hTt = [T(p3, "hTt%d" % i, [128, 8, TT_], BF16) for i in range(2)]
            NA = 8
            Abuf = [T(p3, "A%d" % i, [128, HALF], BF16) for i in range(NA)]
            Bbuf = [T(p3, "B%d" % i, [128, 128], BF16) for i in range(NA)]
            EG = 4
            NUB = 3
            Ub = [T(p3, "Ub%d" % i, [128, EG, 1024], BF16) for i in range(NUB)]
            Vb = [T(p3, "Vb%d" % i, [128, EG, 1024], BF16) for i in range(NUB)]
            ge = [T(p3, "ge%d" % i, [128, TT_], BF16) for i in range(3)]
            wT = [T(p3, "wT%d" % i, [128, TT_], BF16) for i in range(4)]
            h1r = [T(p3, "h1r%d" % i, [128, 1024]) for i in range(2)]
            y2 = T(p3, "y2", [128, 1024])
            ys = [T(p3, "ys%d" % i, [128, 1024]) for i in range(2)]
            bc2 = T(p3, "bc2", [128, 2048])
            S.dma('sp', bc2[:], bc_d[:, 2560:4608].partition_broadcast(128), writes=['bc2'])
            ln2g = bc2[:, 0:1024]
            ln2b = bc2[:, 1024:2048]
            rt = [T(p3, "rt%d" % i, [128, 3, TT_], BF16) for i in range(2)]
            bst2 = T(p3, "bst2", [128, 2, 6])
            mv2 = T(p3, "mv2", [128, 2])
            rstd2 = T(p3, "rstd2", [128, 1])
            ob = [T(p3, "ob%d" % i, [128, 1024]) for i in range(2)]

            po = [[(banks[0], 'bank0'), (banks[1], 'bank1')], [(banks[2], 'bank2'), (banks[3], 'bank3')]]
            pa_rot = Rot([(banks[4], 'bank4'), (banks[5], 'bank5')])
            pg_banks = [(banks[6], 'bank6'), (banks[7], 'bank7')]
            ubv = ubf_d.rearrange("(g p) n -> p g n", p=128)
            vbv = vbf_d.rearrange("(g p) n -> p g n", p=128)
            ngrp = NE1 // EG
            NGG = NT * ngrp

            def load_uv(gg):
                b = gg % NUB
                e0 = (gg % ngrp) * EG
                ch = set([(e0 * 128) // rows_per, ((e0 + EG) * 128 - 1) // rows_per])
                S.dma('sp', Ub[b][:], ubv[:, e0:e0 + EG, :], reads=[('ubf', c) for c in ch], writes=['Ub%d' % b])
                S.dma('sp', Vb[b][:], vbv[:, e0:e0 + EG, :], reads=[('vbf', c) for c in ch], writes=['Vb%d' % b])

            def load_rt(tt_):
                t0_ = tt_ * TT_
                S.dma('sp', rt[tt_ % 2][:], route_d[:, :, t0_:t0_ + TT_],
                      reads=[('route', t0_ // 128 + i) for i in range(TT_ // 128)], writes=['rt%d' % (tt_ % 2)])

            def load_hT(tt_):
                t0_ = tt_ * TT_
                S.dma('sp', hTt[tt_ % 2][:], h1T_d.rearrange("(c p) t -> p c t", p=128)[:, :, t0_:t0_ + TT_],
                      reads=[('h1Td', t0_ // 512)], writes=['hTt%d' % (tt_ % 2)])

            bstate = {'n': 0}

            def build_chunk(tt_, hf, chunk):
                bi = (chunk // 2) % 2
                pg, kg = pg_banks[bi]
                pgv = pg[:].rearrange("p (t e) -> p t e", t=8)
                for i4_ in range(4):
                    tl = chunk * 4 + i4_
                    tg = tt_ * TT_ + tl
                    rk = tg // 128
                    ia = bstate['n'] % NA
                    bstate['n'] += 1
                    r_t = rt[tt_ % 2]
                    krt = 'rt%d' % (tt_ % 2)
                    TS(S, 'dve', Abuf[ia][:], iota_b[:, hf * HALF:(hf + 1) * HALF], r_t[:, 0, tl:tl + 1], r_t[:, 2, tl:tl + 1],
                       ALU.is_equal, ALU.mult, ['consts_b', krt], ['A%d' % ia])
                    TS(S, 'dve', Bbuf[ia][:], iota_b, r_t[:, 1, tl:tl + 1], None, ALU.is_equal, None,
                       ['consts_b', krt], ['B%d' % ia])
                    MM(S, pgv[:, (chunk % 2) * 4 + i4_, :], [(Bbuf[ia][:], Abuf[ia][:])], ['A%d' % ia, 'B%d' % ia], [kg])
                if chunk % 2 == 1:
                    c8 = chunk // 2
                    CP(S, 'act', Gh[hf][:, c8 * 8:(c8 + 1) * 8, :], pgv, [kg], ['G%d' % hf])

            for gg in range(min(3, NGG)):
                load_uv(gg)
            load_hT(0)
            load_rt(0)
            for chunk in range(TT_ // 4):
                build_chunk(0, 0, chunk)

            EQ_prev = None
            for tt in range(NT):
                t0 = tt * TT_
                hT_t = hTt[tt % 2]
                kh = 'hTt%d' % (tt % 2)
                if tt + 1 < NT:
                    load_hT(tt + 1)
                    load_rt(tt + 1)

                def u_stage(e1):
                    gg = tt * ngrp + e1 // EG
                    e1l = e1 % EG
                    b = gg % NUB
                    pa, ka = pa_rot.next()
                    MM(S, pa[:, 0:TT_], [(Ub[b][:, e1l, c * 128:(c + 1) * 128], hT_t[:, c, :]) for c in range(8)],
                       ['Ub%d' % b, kh], [ka])
                    g_t = ge[e1 % 3]
                    kge = 'ge%d' % (e1 % 3)
                    ACT(S, g_t[:], pa[:, 0:TT_], AF.Gelu_apprx_tanh, [ka], [kge])
                    w_t = wT[e1 % 4]
                    kw = 'wT%d' % (e1 % 4)
                    hf = e1 // HALF
                    TT(S, 'pool', w_t[:], g_t[:], Gh[hf][:, :, e1 % HALF], ALU.mult, [kge, 'G%d' % hf], [kw])

                def v_stage(e1):
                    gg = tt * ngrp + e1 // EG
                    e1l = e1 % EG
                    b = gg % NUB
                    w_t = wT[e1 % 4]
                    kw = 'wT%d' % (e1 % 4)
                    for sub in range(2):
                        for half in range(2):
                            pb_, kb_ = po[sub][half]
                            MM(S, pb_[:], [(w_t[:, sub * 128:(sub + 1) * 128], Vb[b][:, e1l, half * 512:(half + 1) * 512])],
                               [kw, 'Vb%d' % b], [kb_], start=(e1 == 0), stop=(e1 == NE1 - 1))
                    if e1l == EG - 1 and gg + 3 < NGG:
                        load_uv(gg + 3)

                u_stage(0)
                for e1 in range(NE1):
                    if EQ_prev is not None and e1 >= 2 and e1 % 2 == 0:
                        if not EQ_prev.empty():
                            EQ_prev.pop_emit()
                    if e1 < HALF:
                        build_chunk(tt, 1, e1)
                    elif tt + 1 < NT:
                        build_chunk(tt + 1, 0, e1 - HALF)
                    if e1 + 1 < NE1:
                        u_stage(e1 + 1)
                    v_stage(e1)

                for sub in range(2):
                    for half in range(2):
                        pb_, kb_ = po[sub][half]
                        CP(S, 'act', ys[sub][:, half * 512:(half + 1) * 512], pb_[:], [kb_], [('ys', sub, half)])
                S_real2 = S
                EQ = Deferred(S_real2)
                S = EQ
                for sub in range(2):
                    r0 = t0 + sub * 128
                    h_r = h1r[sub]
                    khr = 'h1r%d' % sub
                    S.dma('sp', h_r[:], h1_d[r0:r0 + 128, :], reads=[('h1d', r0 // 128)], writes=[khr])
                    for half in range(2):
                        cs_ = slice(half * 512, (half + 1) * 512)
                        STT(S, 'dve', y2[:, cs_], h_r[:, cs_], ALPHA, ys[sub][:, cs_], ALU.mult, ALU.add,
                            [khr, ('ys', sub, half)], [('y2', half)])
                        S.op('dve', lambda E, half=half, cs_=cs_: E.bn_stats(bst2[:, half, :], y2[:, cs_]),
                             [('y2', half)], [('bst2', half)])
                    S.op('dve', lambda E: E.bn_aggr(mv2[:], bst2[:].rearrange("p a b -> p (a b)")),
                         [('bst2', 0), ('bst2', 1)], ['mv2'])
                    TS(S, 'dve', rstd2[:], mv2[:, 1:2], EPS, None, ALU.add, None, ['mv2'], ['rstd2'])
                    ACT(S, rstd2[:], rstd2[:], AF.Sqrt, ['rstd2'], ['rstd2'])
                    S.op('dve', lambda E: E.reciprocal(rstd2[:], rstd2[:]), ['rstd2'], ['rstd2'])
                    o_t = ob[sub]
                    ko = 'ob%d' % sub
                    TS(S, 'dve', o_t[:], y2[:], mv2[:, 0:1], rstd2[:], ALU.subtract, ALU.mult,
                       [('y2', 0), ('y2', 1), 'mv2', 'rstd2'], [ko])
                    TT(S, 'pool', o_t[:], o_t[:], ln2g, ALU.mult, [ko, 'bc2'], [ko])
                    TT(S, 'pool', o_t[:], o_t[:], ln2b, ALU.add, [ko, 'bc2'], [ko])
                    S.dma('sp', out_d[r0:r0 + 128, :], o_t[:], reads=[ko], is_output=True)
                S = S_real2
                if EQ_prev is not None:
                    while not EQ_prev.empty():
                        EQ_prev.pop_emit()
                EQ_prev = EQ
            while not EQ_prev.empty():
                EQ_prev.pop_emit()

        S.finish()
        S.emit()
    return nc


def n_blk(g, s):
    return g * 4 + s


def _prep_shared(inp):
    f32 = np.float32
    w_in = np.asarray(inp["w_in"], f32)
    b_in = np.asarray(inp["b_in"], f32)
    qcols = []
    for c in range(4):
        qcols += list(range(c * 64, c * 64 + 64)) + list(range((4 + c) * 64, (4 + c) * 64 + 64))
    order = qcols + list(range(512, 640)) + list(range(768, 1792))
    w_fm = np.ascontiguousarray(w_in[:, order])
    b_fm = b_in[order].reshape(13, 128).T.copy()
    w_v = np.ascontiguousarray(w_in[:, 640:768])
    rows = np.concatenate([b_in[640:768], np.asarray(inp["b_out"], f32)])[None, :].copy()
    pos = np.arange(S_LEN, dtype=f32)
    inv_freq = (np.float32(500000.0) ** (-np.arange(0, 16, 2, dtype=f32) / np.float32(16))).astype(f32)
    ang = (pos[:, None] * inv_freq[None, :]).astype(f32)
    cosv = np.cos(ang).astype(f32).T
    sinv = np.sin(ang).astype(f32).T
    cos_t = np.ones((128, S_LEN), f32)
    sin_t = np.zeros((128, S_LEN), f32)
    perm = np.zeros((128, 128), f32)
    for blk in range(2):
        o = blk * 64
        cos_t[o:o + 8] = cosv
        cos_t[o + 8:o + 16] = cosv
        sin_t[o:o + 8] = -sinv
        sin_t[o + 8:o + 16] = sinv
        for d in range(8):
            perm[o + d + 8, o + d] = 1.0
            perm[o + d, o + d + 8] = 1.0
    kk = np.arange(128)[:, None]
    qq = np.arange(128)[None, :]
    maskC = (qq >= kk).astype(f32)
    maskP = (kk > qq).astype(f32)
    ident = np.eye(128, dtype=f32)
    iota = np.broadcast_to(np.arange(128, dtype=f32)[None, :], (128, 128))
    consts = np.concatenate([perm, maskC, maskP, ident, iota], axis=1).astype(f32)
    ga = np.asarray(inp["gate_a_w"], f32)
    gx = np.asarray(inp["gate_x_w"], f32)
    w_bd = np.zeros((128, 8, 128), f32)
    for j in range(4):
        for hh_ in range(2):
            w_bd[hh_ * 64:(hh_ + 1) * 64, j, hh_ * 64:(hh_ + 1) * 64] = ga[2 * j + hh_]
            w_bd[hh_ * 64:(hh_ + 1) * 64, 4 + j, hh_ * 64:(hh_ + 1) * 64] = gx[2 * j + hh_]
    rnnp = np.zeros((128, 4, 9), f32)
    cw = np.asarray(inp["conv_w"], f32)
    for k in range(4):
        rnnp[:, :, k] = cw[k].reshape(4, 128).T
    for idx, nm in [(4, "conv_b"), (5, "gate_a_b"), (6, "gate_x_b"), (7, "lru_lambda"), (8, "norm_rnn_g")]:
        rnnp[:, :, idx] = np.asarray(inp[nm], f32).reshape(4, 128).T
    bcast = np.concatenate([np.asarray(inp[n], f32) for n in
                            ["norm_attn_g", "ln1_g", "ln1_b", "ln2_g", "ln2_b", "attn_sinks"]])[None, :].copy()
    keysT = np.concatenate([np.asarray(inp["peer_keys_1"], f32).T, np.asarray(inp["peer_keys_2"], f32).T], axis=1).copy()
    pu = np.asarray(inp["peer_u"], f32)
    u_l = np.ascontiguousarray(pu.reshape(128, 128, 8, 128).transpose(0, 3, 2, 1)).reshape(16384, 1024)
    return {
        "w_fm": w_fm, "w_v": w_v, "b_fm": b_fm, "rows": rows, "cos_t": cos_t, "sin_t": sin_t, "consts": consts,
        "w_bd": w_bd.reshape(128, 1024), "rnnp": rnnp.reshape(128, 36), "bcast": bcast,
        "w_out": np.ascontiguousarray(np.asarray(inp["w_out"], f32)),
        "w_q": np.ascontiguousarray(np.asarray(inp["peer_w_q"], f32)), "keysT": keysT,
        "u_l": u_l, "v_l": np.ascontiguousarray(np.asarray(inp["peer_v"], f32)),
    }


_NC_CACHE = {}


def kernel(**inputs):
    x = np.asarray(inputs["x"], np.float32)
    shared = _prep_shared(inputs)
    if "nc" not in _NC_CACHE:
        _NC_CACHE["nc"] = build_program()
    nc = _NC_CACHE["nc"]
    in_maps = []
    for b in range(NCORES):
        m = dict(shared)
        m["x_tok"] = np.ascontiguousarray(x[b])
        m["xT"] = np.ascontiguousarray(x[b].T)
        in_maps.append(m)
    res = run_bass_kernel_spmd(nc, in_maps, core_ids=list(range(NCORES)))
    out = np.stack([np.asarray(r["out"], np.float32) for r in res.results], axis=0)
    if DEBUG:
        kernel.dbg = [{k: np.asarray(v) for k, v in r.items() if k.startswith("dbg")} for r in res.results]
    return out
```
